# Optimizing a Trainium2 kernel written in Bass

```python
import jax
import jax.numpy as jnp
from jax import lax
import numpy as np

D_MODEL = 4096
BATCH = 1
SEQ = 16384
DEPTH = 2
DEC_BATCH = 16
DEC_SEQ = 16
PAST_LEN = 2048

CHUNK = 64
N_A = DEPTH // 2
N_B = DEPTH - N_A
D_MIX = D_MODEL
D_MEMQ = D_MIX // 4
MEM_HEADS = 4
MEM_HD = D_MEMQ // MEM_HEADS
N_MEM = 256
D_POOL = D_MIX - D_MEMQ
POOL_WINDOWS = (2, 4, 8, 16)
POOL_GROUP = D_POOL // len(POOL_WINDOWS)
POOL_STATE = max(POOL_WINDOWS) - 1
HD_FOX = 128
H_FOX = (D_MIX - D_MEMQ) // HD_FOX
D_FOX = H_FOX * HD_FOX
D_FF = 11008
Q_BLOCK = 128
EPS = 1e-6
N_NORMS = 6
FORGET_BIAS = 2.0

kernel_name = "yoco_pool_fox_macaron_stream_step"


def rmsnorm(x, g):
    xf = x.astype(jnp.float32)
    y = xf * lax.rsqrt(jnp.mean(xf * xf, axis=-1, keepdims=True) + EPS)
    return (y * g.astype(jnp.float32)).astype(x.dtype)


def swiglu(h, wg, wu, wd):
    return (jax.nn.silu(h @ wg) * (h @ wu)) @ wd


def pool_mixer(u, u_prev, pos, pool_w, pool_scale):
    b, n, _ = u.shape
    ext = jnp.concatenate([u_prev.astype(u.dtype), u], axis=1).astype(jnp.float32)
    cs = jnp.cumsum(jnp.pad(ext, ((0, 0), (1, 0), (0, 0))), axis=1)
    end = cs[:, POOL_STATE + 1:]
    outs = []
    for gi, w in enumerate(POOL_WINDOWS):
        sl = slice(gi * POOL_GROUP, (gi + 1) * POOL_GROUP)
        start = cs[:, POOL_STATE + 1 - w:POOL_STATE + 1 - w + n, sl]
        cnt = jnp.minimum(pos + 1, w).astype(jnp.float32)[None, :, None]
        outs.append((end[..., sl] - start) / cnt)
    pooled = jnp.stack(outs, axis=2)
    diff = pooled - u.reshape(b, n, len(POOL_WINDOWS), POOL_GROUP).astype(jnp.float32)
    y = jnp.einsum('bngc,gcd->bngd', diff.astype(pool_w.dtype), pool_w).reshape(b, n, D_POOL)
    y = (y * pool_scale).astype(u.dtype)
    return y, ext[:, -POOL_STATE:].astype(u.dtype)


def fox_block(q, fq, pos_q, k, v, fk, pos_k):
    s = jnp.einsum('bqhd,bkhd->bhqk', q, k).astype(jnp.float32) * (HD_FOX ** -0.5)
    s = s + jnp.transpose(fq, (0, 2, 1))[..., :, None] - jnp.transpose(fk, (0, 2, 1))[..., None, :]
    s = jnp.where(pos_k[None, :] <= pos_q[:, None], s, -jnp.inf)
    p = jax.nn.softmax(s, axis=-1)
    return jnp.einsum('bhqk,bkhd->bqhd', p.astype(v.dtype), v)


def fox_attend(q, fq, pos_q, k, v, fk, pos_k):
    b, n, h, d = q.shape
    if n <= Q_BLOCK:
        return fox_block(q, fq, pos_q, k, v, fk, pos_k).reshape(b, n, h * d)
    nb = n // Q_BLOCK
    qb = q.reshape(b, nb, Q_BLOCK, h, d).transpose(1, 0, 2, 3, 4)
    fb = fq.reshape(b, nb, Q_BLOCK, h).transpose(1, 0, 2, 3)
    pb = pos_q.reshape(nb, Q_BLOCK)
    out = lax.map(lambda a: fox_block(a[0], a[1], a[2], k, v, fk, pos_k), (qb, fb, pb))
    return out.transpose(1, 0, 2, 3, 4).reshape(b, n, h * d)


def mem_attend(q, mk, mv):
    b, n = q.shape[:2]
    s = jnp.einsum('bqhd,bkhd->bhqk', q, mk).astype(jnp.float32) * (MEM_HD ** -0.5)
    p = jax.nn.softmax(s, axis=-1)
    return jnp.einsum('bhqk,bkhd->bqhd', p.astype(mv.dtype), mv).reshape(b, n, D_MEMQ)


def memory_kv(mem, g_mem, w_mem_kv):
    b, m, _ = mem.shape
    ks, vs = [], []
    for l in range(DEPTH):
        kv = rmsnorm(mem, g_mem[l]) @ w_mem_kv[l]
        ks.append(kv[..., :D_MEMQ].reshape(b, m, MEM_HEADS, MEM_HD))
        vs.append(kv[..., D_MEMQ:].reshape(b, m, MEM_HEADS, MEM_HD))
    return jnp.stack(ks), jnp.stack(vs)


def trunk(x, pool_prev, mem_k, mem_v, k_past, v_past, logf_past,
          norm_g, w_ffn_gate, w_ffn_up, w_ffn_down, w_in_a, pool_w, pool_scale, w_out_a,
          w_in_b, w_out_b, g_kv, w_kv, w_f, b_f):
    b, n, _ = x.shape
    p = k_past.shape[1]
    pos = p + jnp.arange(n, dtype=jnp.int32)
    pos_k = jnp.arange(p + n, dtype=jnp.int32)
    new_pool = []
    k_new = v_new = logf_new = None
    k_all = v_all = f_all = None
    for l in range(DEPTH):
        g = norm_g[l]
        x = x + 0.5 * rmsnorm(swiglu(rmsnorm(x, g[0]), w_ffn_gate[l, 0], w_ffn_up[l, 0], w_ffn_down[l, 0]), g[1])
        h = rmsnorm(x, g[2])
        if l < N_A:
            proj = h @ w_in_a[l]
            main, u_last = pool_mixer(proj[..., :D_POOL], pool_prev[l], pos, pool_w[l], pool_scale[l])
            new_pool.append(u_last)
            w_out = w_out_a[l]
        else:
            j = l - N_A
            proj = h @ w_in_b[j]
            q = proj[..., :D_FOX].reshape(b, n, H_FOX, HD_FOX)
            main = fox_attend(q, f_all[:, p:], pos, k_all, v_all, f_all, pos_k)
            w_out = w_out_b[j]
        qm = proj[..., D_MIX - D_MEMQ:].reshape(b, n, MEM_HEADS, MEM_HD)
        mixed = jnp.concatenate([main, mem_attend(qm, mem_k[l], mem_v[l])], axis=-1) @ w_out
        x = x + rmsnorm(mixed, g[3])
        x = x + 0.5 * rmsnorm(swiglu(rmsnorm(x, g[4]), w_ffn_gate[l, 1], w_ffn_up[l, 1], w_ffn_down[l, 1]), g[5])
        if l == N_A - 1:
            hk = rmsnorm(x, g_kv)
            kv = hk @ w_kv
            k_new = kv[..., :D_FOX].reshape(b, n, H_FOX, HD_FOX)
            v_new = kv[..., D_FOX:].reshape(b, n, H_FOX, HD_FOX)
            logf_new = jax.nn.log_sigmoid((hk @ w_f + b_f).astype(jnp.float32))
            k_all = jnp.concatenate([k_past.astype(k_new.dtype), k_new], axis=1)
            v_all = jnp.concatenate([v_past.astype(v_new.dtype), v_new], axis=1)
            f_all = jnp.cumsum(jnp.concatenate([logf_past.astype(jnp.float32), logf_new], axis=1), axis=1)
    return x, jnp.stack(new_pool), k_new, v_new, logf_new


def setup_inputs(seed: int = 0) -> dict:
    key = jax.random.key(seed)
    ks = jax.random.split(key, 26)

    def nrm(k, shape, scale):
        return scale * jax.random.normal(k, shape, jnp.float32)

    return {
        "x_prompt": nrm(ks[0], (BATCH, SEQ, D_MODEL), 1.0),
        "x_sample": nrm(ks[1], (DEC_BATCH, DEC_SEQ, D_MODEL), 1.0),
        "mem_prompt": nrm(ks[2], (BATCH, N_MEM, D_MODEL), 1.0),
        "cache_fox_k": nrm(ks[3], (DEC_BATCH, PAST_LEN, H_FOX, HD_FOX), 1.0),
        "cache_fox_v": nrm(ks[4], (DEC_BATCH, PAST_LEN, H_FOX, HD_FOX), 1.0),
        "cache_fox_logf": jax.nn.log_sigmoid(FORGET_BIAS + nrm(ks[5], (DEC_BATCH, PAST_LEN, H_FOX), 0.5)),
        "cache_mem_k": nrm(ks[6], (DEPTH, DEC_BATCH, N_MEM, MEM_HEADS, MEM_HD), 1.0),
        "cache_mem_v": nrm(ks[7], (DEPTH, DEC_BATCH, N_MEM, MEM_HEADS, MEM_HD), 1.0),
        "state_pool": nrm(ks[8], (N_A, DEC_BATCH, POOL_STATE, D_POOL), 1.0),
        "norm_g": 1.0 + nrm(ks[9], (DEPTH, N_NORMS, D_MODEL), 0.05),
        "w_ffn_gate": nrm(ks[10], (DEPTH, 2, D_MODEL, D_FF), D_MODEL ** -0.5),
        "w_ffn_up": nrm(ks[11], (DEPTH, 2, D_MODEL, D_FF), D_MODEL ** -0.5),
        "w_ffn_down": nrm(ks[12], (DEPTH, 2, D_FF, D_MODEL), D_FF ** -0.5),
        "w_in_a": nrm(ks[13], (N_A, D_MODEL, D_MIX), D_MODEL ** -0.5),
        "pool_w": nrm(ks[14], (N_A, len(POOL_WINDOWS), POOL_GROUP, POOL_GROUP), POOL_GROUP ** -0.5),
        "pool_scale": 1.0 + nrm(ks[15], (N_A, D_POOL), 0.05),
        "w_out_a": nrm(ks[16], (N_A, D_MIX, D_MODEL), D_MIX ** -0.5),
        "w_in_b": nrm(ks[17], (N_B, D_MODEL, D_MIX), D_MODEL ** -0.5),
        "w_out_b": nrm(ks[18], (N_B, D_MIX, D_MODEL), D_MIX ** -0.5),
        "g_kv": 1.0 + nrm(ks[19], (D_MODEL,), 0.05),
        "w_kv": nrm(ks[20], (D_MODEL, 2 * D_FOX), D_MODEL ** -0.5),
        "w_f": nrm(ks[21], (D_MODEL, H_FOX), 0.5 * D_MODEL ** -0.5),
        "b_f": FORGET_BIAS + nrm(ks[22], (H_FOX,), 0.5),
        "g_mem": 1.0 + nrm(ks[23], (DEPTH, D_MODEL), 0.05),
        "w_mem_kv": nrm(ks[24], (DEPTH, D_MODEL, 2 * D_MEMQ), D_MODEL ** -0.5),
    }


def reference(x_prompt, x_sample, mem_prompt, cache_fox_k, cache_fox_v, cache_fox_logf,
              cache_mem_k, cache_mem_v, state_pool, norm_g, w_ffn_gate, w_ffn_up, w_ffn_down,
              w_in_a, pool_w, pool_scale, w_out_a, w_in_b, w_out_b, g_kv, w_kv, w_f, b_f,
              g_mem, w_mem_kv):
    b = x_prompt.shape[0]
    mem_k_p, mem_v_p = memory_kv(mem_prompt, g_mem, w_mem_kv)
    y_p, pool_p, k_p, v_p, logf_p = trunk(
        x_prompt,
        jnp.zeros((N_A, b, POOL_STATE, D_POOL), x_prompt.dtype),
        mem_k_p, mem_v_p,
        jnp.zeros((b, 0, H_FOX, HD_FOX), x_prompt.dtype),
        jnp.zeros((b, 0, H_FOX, HD_FOX), x_prompt.dtype),
        jnp.zeros((b, 0, H_FOX), jnp.float32),
        norm_g, w_ffn_gate, w_ffn_up, w_ffn_down, w_in_a, pool_w, pool_scale, w_out_a,
        w_in_b, w_out_b, g_kv, w_kv, w_f, b_f)
    y_s, pool_s, k_s, v_s, logf_s = trunk(
        x_sample, state_pool, cache_mem_k, cache_mem_v, cache_fox_k, cache_fox_v, cache_fox_logf,
        norm_g, w_ffn_gate, w_ffn_up, w_ffn_down, w_in_a, pool_w, pool_scale, w_out_a,
        w_in_b, w_out_b, g_kv, w_kv, w_f, b_f)
    return (y_p, y_s, k_p, v_p, logf_p, mem_k_p, mem_v_p, pool_p, k_s, v_s, logf_s, pool_s)
```

```python
import numpy as np
import concourse.bass as bass
import concourse.mybir as mybir
from concourse.bass_utils import run_bass_kernel_spmd
from contextlib import ExitStack

F32 = mybir.dt.float32
BF16 = mybir.dt.bfloat16
AF = mybir.ActivationFunctionType
ALU = mybir.AluOpType
AX = mybir.AxisListType

D = 4096
NKC = 32
DFF = 11008
NF = 86
GF = 8
DPOOL = 3072
NH = 24
EPS = 1e-6
R = 5
NCORES = 8
TP = 2048
TE_A = 47
TE_B = 32
PAST = 2048
BIG = 30000.0


def p3(ap2d):
    return ap2d.rearrange("(c p) t -> p c t", p=128)


class Ctx:
    def __init__(s, nf=NF):
        s.nc = bass.Bass("TRN2", target_bir_lowering=False)
        s.st = ExitStack()
        nc = s.nc
        s.NF = nf
        s.eng = {'pe': nc.tensor, 'act': nc.scalar, 'dve': nc.vector, 'pool': nc.gpsimd, 'sp': nc.sync}
        s.pg = {e: s.sem("pg_" + e) for e in ('pe', 'act', 'dve', 'pool')}
        s.wf = s.sem("wfree")
        s.waited = {}
        s.acc = s.sb("acc", [128, 16384], F32)
        s.hT = s.sb("hT", [128, 16384], BF16)
        s.aT = s.sb("aT", [128, 8192], BF16)
        s.ring = [s.sb(f"ring{i}", [128, 4096], BF16) for i in range(R)]
        s.rld = [s.sem(f"rld{i}") for i in range(R)]
        s.rn = 0
        s.xres = s.sb("xres", [128, 4, 512], F32)
        s.rstd = s.sb("rstd", [128, 512], F32)
        s.tmp32 = s.sb("tmp32", [128, 2, 512], F32)
        s.sg = s.sb("sg", [128, 2, 512], F32)
        s.gv = s.sb("gv", [128, 7 * 32], F32)
        s.gvh = s.sb("gvh", [128, 7 * 32], F32)
        s.ones = s.sb("ones", [128, 128], BF16)
        s.negones = s.sb("negones", [128, 128], BF16)
        s.onecol = s.sb("onecol", [128, 1], F32)
        s.ps = [s.st.enter_context(nc.psum_tensor(f"ps{i}", [128, 512], F32)) for i in range(8)]
        s.bankfree = [None] * 8
        s.xq = [s.sem("xq0"), s.sem("xq1")]
        s.stq = s.sem("stq")
        s.sq = [s.sem("sq0"), s.sem("sq1")]
        s.cq = s.sem("cq")
        s.oq = s.sem("oq")
        s.acc_ready = None
        s.hT_free = None
        s.hT_ready = None
        s.xres_free = [None, None]
        s.tmp_free = [None, None]
        s.sg_free = [None, None]
        s.aT_free = [None, None]
        s.rstd_free = None
        s.acc_free = None

    def sem(s, name):
        return [s.st.enter_context(s.nc.semaphore(name)), 0]

    def sb(s, name, shape, dt):
        return s.st.enter_context(s.nc.sbuf_tensor(name, shape, dt))

    def mark(s, e, ins):
        p = s.pg[e]
        p[1] += 1
        ins.then_inc(p[0], 1)
        return (p, p[1])

    def wait(s, cons, tok):
        if tok is None:
            return
        so, v = tok
        key = (cons, id(so))
        if s.waited.get(key, 0) >= v:
            return
        s.waited[key] = v
        s.eng[cons].wait_ge(so[0], v)

    def dma(s, q, eng, out, in_):
        ins = s.eng[eng].dma_start(out=out, in_=in_)
        q[1] += 16
        ins.then_inc(q[0], 16)
        return (q, q[1])

    def mm(s, out, lhsT, rhs, start, stop):
        return s.nc.tensor.matmul(out, lhsT=lhsT, rhs=rhs, start=start, stop=stop)

    def rget(s, src, k, w):
        n = s.rn
        s.rn += 1
        i = n % R
        if n >= R:
            s.wait('pool', (s.wf, n - R + 1))
        view = s.ring[i][:, 0:k * w].rearrange("p (k w) -> p k w", w=w)
        tok = s.dma(s.rld[i], 'pool', view, src)
        s.wait('pe', tok)
        return view

    def rrel(s, ins):
        s.wf[1] += 1
        ins.then_inc(s.wf[0], 1)
        return (s.wf, s.wf[1])

    def acc3(s, T):
        return s.acc[:, 0:32 * T].rearrange("p (c t) -> p c t", t=T)

    def hT3(s, T):
        return s.hT[:, 0:32 * T].rearrange("p (c t) -> p c t", t=T)

    def consts(s, gsrc):
        nc = s.nc
        t = s.dma(s.cq, 'sp', s.gv[:], gsrc)
        nc.vector.memset(s.ones[:], 1.0)
        nc.vector.memset(s.negones[:], -1.0)
        nc.vector.memset(s.onecol[:], 1.0)
        s.wait('dve', t)
        tk = s.mark('dve', nc.vector.tensor_scalar(out=s.gvh[:], in0=s.gv[:], scalar1=0.5, scalar2=None, op0=ALU.mult))
        for e in ('pe', 'act', 'dve', 'pool'):
            s.wait(e, tk)

    def stats(s, T, tsq):
        nc = s.nc
        hT = s.hT3(T)
        s.wait('pe', tsq)
        s.wait('pe', s.bankfree[6])
        for ch in range(32):
            ins = s.mm(s.ps[6][:, :T], s.ones[:, :], hT[:, ch, :], ch == 0, ch == 31)
        tss = s.mark('pe', ins)
        s.hT_free = tss
        s.wait('dve', tss)
        s.wait('dve', s.rstd_free)
        t1 = s.mark('dve', nc.vector.tensor_scalar(out=s.rstd[:, :T], in0=s.ps[6][:, :T], scalar1=1.0 / D, scalar2=EPS,
                                                   op0=ALU.mult, op1=ALU.add))
        s.bankfree[6] = t1
        s.wait('act', t1)
        t2 = s.mark('act', nc.scalar.activation(out=s.rstd[:, :T], in_=s.rstd[:, :T], func=AF.Sqrt))
        s.wait('dve', t2)
        t3 = s.mark('dve', nc.vector.reciprocal(out=s.rstd[:, :T], in_=s.rstd[:, :T]))
        s.wait('dve', t3)
        return t3

    def squares(s, T):
        nc = s.nc
        acc = s.acc3(T)
        hT = s.hT3(T)
        s.wait('act', s.acc_ready)
        s.wait('act', s.hT_free)
        for ch in range(32):
            ins = nc.scalar.activation(out=hT[:, ch, :], in_=acc[:, ch, :], func=AF.Square)
        return s.mark('act', ins)

    def prenorm(s, T, gi):
        nc = s.nc
        acc = s.acc3(T)
        hT = s.hT3(T)
        tsq = s.squares(T)
        s.stats(T, tsq)
        s.wait('dve', s.acc_ready)
        for ch in range(32):
            ins = nc.vector.scalar_tensor_tensor(out=hT[:, ch, :], in0=acc[:, ch, :],
                                                 scalar=s.gv[:, gi * 32 + ch: gi * 32 + ch + 1], in1=s.rstd[:, :T],
                                                 op0=ALU.mult, op1=ALU.mult)
        s.hT_ready = s.mark('dve', ins)
        s.rstd_free = s.hT_ready
        s.wait('pe', s.hT_ready)

    def postnorm(s, T, gi, half, xsrc, dsts):
        nc = s.nc
        acc = s.acc3(T)
        tsq = s.squares(T)
        s.stats(T, tsq)
        gt = s.gvh if half else s.gv
        s.wait('sp', (s.stq, s.stq[1]))
        t2 = None
        for pr in range(16):
            b = pr % 2
            s.wait('sp', s.xres_free[b])
            tl = s.dma(s.xq[b], 'sp', s.xres[:, 2 * b:2 * b + 2, :T], p3(xsrc[pr * 256:(pr + 1) * 256, :]))
            for k in range(2):
                ch = 2 * pr + k
                s.wait('dve', tl)
                s.wait('dve', s.tmp_free[k])
                t1 = s.mark('dve', nc.vector.scalar_tensor_tensor(out=s.tmp32[:, k, :T], in0=acc[:, ch, :],
                                                                  scalar=gt[:, gi * 32 + ch: gi * 32 + ch + 1],
                                                                  in1=s.rstd[:, :T], op0=ALU.mult, op1=ALU.mult))
                s.wait('dve', t1)
                t2 = s.mark('dve', nc.vector.tensor_tensor(out=acc[:, ch, :], in0=s.tmp32[:, k, :T],
                                                           in1=s.xres[:, 2 * b + k, :T], op=ALU.add))
            s.xres_free[b] = t2
        s.acc_ready = t2
        s.rstd_free = t2
        s.wait('sp', t2)
        for dst in dsts:
            s.dma(s.stq, 'sp', p3(dst), acc)
        s.acc_free = (s.stq, s.stq[1])

    def load_x(s, T, src):
        s.wait('sp', s.acc_free)
        s.wait('sp', s.hT_ready)
        s.wait('sp', s.acc_ready)
        s.acc_ready = s.dma(s.xq[0], 'sp', s.acc3(T), p3(src))

    def proj(s, Wfn, n_k, rhs, T, n_oc, evac, banks=(0, 1)):
        for oc in range(n_oc):
            b = banks[oc % len(banks)]
            w = s.rget(Wfn(oc), n_k, 128)
            s.wait('pe', s.bankfree[b])
            for kc in range(n_k):
                ins = s.mm(s.ps[b][:, :T], w[:, kc, :], rhs(kc), kc == 0, kc == n_k - 1)
            tok = s.rrel(ins)
            s.bankfree[b] = evac(oc, s.ps[b][:, :T], tok)
        return tok

    def ffn(s, T, wg, wu, wd, gi_pre, gi_post, xsrc, dsts):
        nc = s.nc
        acc = s.acc3(T)
        hT = s.hT3(T)
        s.prenorm(T, gi_pre)
        wgv = wg.rearrange("(kc p) f -> p kc f", p=128)
        wuv = wu.rearrange("(kc p) f -> p kc f", p=128)
        wdv = wd.rearrange("(fc p) d -> p fc d", p=128)
        aT = s.aT[:, 0:2 * GF * T].rearrange("p (b j t) -> p b j t", b=2, j=GF)
        NFF = s.NF
        ng = (NFF + GF - 1) // GF
        s.wait('dve', s.acc_free)
        for g in range(ng):
            g0 = g * GF
            G = min(GF, NFF - g0)
            buf = g % 2
            for j in range(G):
                f = g0 + j
                bg = f % 2
                bu = 2 + f % 2
                w = s.rget(wgv[:, :, f * 128:(f + 1) * 128], 32, 128)
                s.wait('pe', s.bankfree[bg])
                for kc in range(32):
                    ins = s.mm(s.ps[bg][:, :T], w[:, kc, :], hT[:, kc, :], kc == 0, kc == 31)
                tg = s.rrel(ins)
                w = s.rget(wuv[:, :, f * 128:(f + 1) * 128], 32, 128)
                s.wait('pe', s.bankfree[bu])
                for kc in range(32):
                    ins = s.mm(s.ps[bu][:, :T], w[:, kc, :], hT[:, kc, :], kc == 0, kc == 31)
                tu = s.rrel(ins)
                s.wait('act', tg)
                s.wait('act', s.sg_free[f % 2])
                ta = s.mark('act', nc.scalar.activation(out=s.sg[:, f % 2, :T], in_=s.ps[bg][:, :T], func=AF.Silu))
                s.bankfree[bg] = ta
                s.wait('dve', tu)
                s.wait('dve', ta)
                s.wait('dve', s.aT_free[buf])
                tm = s.mark('dve', nc.vector.tensor_tensor(out=aT[:, buf, j, :], in0=s.sg[:, f % 2, :T],
                                                           in1=s.ps[bu][:, :T], op=ALU.mult))
                s.bankfree[bu] = tm
                s.sg_free[f % 2] = tm
            for dg in range(8):
                w = s.rget(wdv[:, g0:g0 + G, dg * 512:(dg + 1) * 512], G, 512)
                s.wait('pe', tm)
                for dc in range(4):
                    d = dg * 4 + dc
                    bd = 4 + d % 2
                    s.wait('pe', s.bankfree[bd])
                    for j in range(G):
                        ins = s.mm(s.ps[bd][:, :T], w[:, j, dc * 128:(dc + 1) * 128], aT[:, buf, j, :], j == 0, j == G - 1)
                    td = s.rrel(ins) if dc == 3 else s.mark('pe', ins)
                    s.wait('dve', td)
                    if g == 0:
                        te = s.mark('dve', nc.vector.tensor_copy(out=acc[:, d, :], in_=s.ps[bd][:, :T]))
                    else:
                        te = s.mark('dve', nc.vector.tensor_tensor(out=acc[:, d, :], in0=acc[:, d, :],
                                                                   in1=s.ps[bd][:, :T], op=ALU.add))
                    s.bankfree[bd] = te
            s.aT_free[buf] = td
        s.hT_free = tu
        s.acc_ready = te
        s.postnorm(T, gi_post, True, xsrc, dsts)


    def setup_mem(s, memT, wmem, gi, memkT_out, memvT_out):
        nc = s.nc
        T = 256
        s.load_x(T, memT)
        s.prenorm(T, gi)
        hT = s.hT3(T)
        wv = wmem.rearrange("(kc p) f -> p kc f", p=128)
        vT = s.aT[:, 0:8 * 256].rearrange("p (c t) -> p c t", t=256)

        def evac(oc, ps, tok):
            s.wait('act', tok)
            b = oc % 2
            s.wait('act', s.tmp_free[b])
            if oc < 8:
                nc.scalar.copy(out=s.mkT[:, oc, :], in_=ps)
            else:
                nc.scalar.copy(out=vT[:, oc - 8, :], in_=ps)
            t = s.mark('act', nc.scalar.copy(out=s.tmp32[:, b, :T], in_=ps))
            s.wait('sp', t)
            dst = (memkT_out if oc < 8 else memvT_out)[(oc % 8) * 128:(oc % 8 + 1) * 128, :]
            s.tmp_free[b] = s.dma(s.sq[b], 'sp', dst, s.tmp32[:, b, :T])
            return t
        tk = s.proj(lambda oc: wv[:, :, oc * 128:(oc + 1) * 128], 32, lambda kc: hT[:, kc, :], T, 16, evac)
        s.hT_free = tk
        last = s.bankfree[0] if s.bankfree[0][1] > s.bankfree[1][1] else s.bankfree[1]
        s.wait('pe', last)
        for kt in range(2):
            s.wait('pe', s.bankfree[7])
            for ch in range(8):
                ins = nc.tensor.transpose(s.psb[:, ch * 128:(ch + 1) * 128], vT[:, ch, kt * 128:(kt + 1) * 128], s.ident[:, :])
            t = s.mark('pe', ins)
            s.wait('dve', t)
            t = s.mark('dve', nc.vector.tensor_copy(out=s.mV[:, kt, :], in_=s.psb[:, 0:1024]))
            s.bankfree[7] = t
        for e in ('pe', 'act', 'dve', 'pool'):
            s.wait(e, t)
        s.aT_free = [t, t]
        s.acc_free = t

    def mem_attn(s, T, qm, cols, kT, V, outT, kmx):
        nc = s.nc
        c0, n = cols
        scale = 1.0 / 16.0
        for h in range(4):
            s.wait('act', s.pT_free[0])
            for i in range(2):
                ins = nc.scalar.activation(out=s.pT[:, i, :n], in_=qm[:, 2 * h + i, c0:c0 + n], func=AF.Square)
            t = s.mark('act', ins)
            s.wait('pe', t)
            s.wait('pe', s.bankfree[6])
            for i in range(2):
                ins = s.mm(s.ps[6][0:1, :n], s.ones[:, 0:1], s.pT[:, i, :n], i == 0, i == 1)
            t = s.mark('pe', ins)
            s.pT_free[0] = t
            s.wait('act', t)
            s.wait('act', s.mrow_free)
            t = s.mark('act', nc.scalar.activation(out=s.mrow[0:1, :n], in_=s.ps[6][0:1, :n], func=AF.Sqrt, scale=kmx[0:1, h:h + 1]))
            s.bankfree[6] = t
            s.wait('pe', t)
            tl = None
            for kt in range(2):
                b = kt
                s.wait('pe', s.bankfree[b])
                for i in range(2):
                    s.mm(s.ps[b][:, :n], kT[:, 2 * h + i, kt * 128:(kt + 1) * 128], qm[:, 2 * h + i, c0:c0 + n], i == 0, False)
                ins = s.mm(s.ps[b][:, :n], s.negones[0:1, :], s.mrow[0:1, :n], False, True)
                t = s.mark('pe', ins)
                s.wait('act', t)
                s.wait('act', s.pT_free[kt])
                te = s.mark('act', nc.scalar.activation(out=s.pT[:, kt, :n], in_=s.ps[b][:, :n], func=AF.Exp, scale=scale))
                s.bankfree[b] = te
            s.mrow_free = t
            s.wait('pe', te)
            for i in range(2):
                s.wait('pe', s.bankfree[2 + i])
                for kt in range(2):
                    ins = s.mm(s.ps[2 + i][:, :n], V[:, kt, h * 256 + i * 128: h * 256 + (i + 1) * 128], s.pT[:, kt, :n], kt == 0, kt == 1)
            s.wait('pe', s.bankfree[4])
            for kt in range(2):
                ins = s.mm(s.ps[4][:, :n], s.ones[:, :], s.pT[:, kt, :n], kt == 0, kt == 1)
            t = s.mark('pe', ins)
            s.pT_free = [t, t, t]
            s.wait('dve', t)
            s.wait('dve', s.rstd_free)
            t1 = s.mark('dve', nc.vector.reciprocal(out=s.rl[:, :n], in_=s.ps[4][:, :n]))
            s.bankfree[4] = t1
            s.wait('dve', t1)
            for i in range(2):
                t2 = s.mark('dve', nc.vector.tensor_tensor(out=outT(h, i), in0=s.ps[2 + i][:, :n], in1=s.rl[:, :n], op=ALU.mult))
                s.bankfree[2 + i] = t2
            s.rstd_free = t2
        return t2

    def kmax(s, kT, kmx):
        nc = s.nc
        for h in range(4):
            s.wait('act', s.pT_free[0])
            for i in range(2):
                ins = nc.scalar.activation(out=s.pT[:, i, :256], in_=kT[:, 2 * h + i, :], func=AF.Square)
            t = s.mark('act', ins)
            s.wait('pe', t)
            s.wait('pe', s.bankfree[6])
            for i in range(2):
                ins = s.mm(s.ps[6][:, :256], s.ones[:, :], s.pT[:, i, :256], i == 0, i == 1)
            t = s.mark('pe', ins)
            s.pT_free[0] = t
            s.wait('dve', t)
            t = s.mark('dve', nc.vector.tensor_reduce(out=kmx[:, h:h + 1], in_=s.ps[6][:, :256], axis=AX.X, op=ALU.max))
            s.bankfree[6] = t
        for e in ('act', 'pe', 'dve'):
            s.wait(e, t)

    def mixer_a(s, T, seqs, memsets, w_in, pool_w, w_out, xsrc, dsts, first_main, poolT_hist, poolT_out_s, poolT_out_p, last_main):
        nc = s.nc
        s.prenorm(T, 2)
        hT = s.hT3(T)
        W = sum(16 + n for (_, n, _) in seqs)
        uext = s.acc[:, 0:24 * W].rearrange("p (c w) -> p c w", w=W)
        bases = []
        b0 = 0
        for (_, n, _) in seqs:
            bases.append(b0)
            b0 += 16 + n
        wv = w_in.rearrange("(kc p) f -> p kc f", p=128)
        s.wait('dve', s.hT_ready)
        s.wait('dve', s.acc_free)
        th = None
        for si, (c0, n, hist) in enumerate(seqs):
            b = bases[si]
            if hist == 'carry':
                th = s.mark('dve', nc.vector.tensor_copy(out=uext[:, :, b + 1:b + 16], in_=s.carry[:, :, 1:16]))
            elif hist is None:
                th = s.mark('dve', nc.vector.memset(uext[:, :, b:b + 16], 0.0))
            else:
                s.wait('sp', s.hT_ready)
                s.wait('sp', s.acc_free)
                with nc.allow_non_contiguous_dma(reason="pool history"):
                    th = s.dma(s.cq, 'sp', uext[:, :, b + 1:b + 16], poolT_hist[:, hist, :].rearrange("(c p) j -> p c j", p=128))
                s.wait('dve', th)

        def evac_in(oc, ps, tok):
            s.wait('act', tok)
            if oc < 24:
                for si, (c0, n, hist) in enumerate(seqs):
                    b = bases[si]
                    ins = nc.scalar.copy(out=uext[:, oc, b + 16:b + 16 + n], in_=ps[:, c0:c0 + n])
            else:
                ins = nc.scalar.copy(out=s.qm[:, oc - 24, :T], in_=ps)
            return s.mark('act', ins)
        s.wait('act', s.hT_ready)
        s.wait('act', s.acc_free)
        s.wait('act', s.qm_free)
        tk = s.proj(lambda oc: wv[:, :, oc * 128:(oc + 1) * 128], 32, lambda kc: hT[:, kc, :], T, 32, evac_in)
        s.hT_free = tk
        tin = s.bankfree[1] if s.bankfree[1][1] > s.bankfree[0][1] else s.bankfree[0]
        mixT = s.hT3(T)
        s.wait('dve', tk)
        s.wait('act', tk)
        tm = None
        for (cols, kT, V, kmx) in memsets:
            c0, n = cols
            tm = s.mem_attn(T, s.qm, cols, kT, V, lambda h, i, c0=c0, n=n: mixT[:, 24 + 2 * h + i, c0:c0 + n], kmx)
        s.qm_free = tm
        covered = sorted(cols for (cols, _, _, _) in memsets)
        if covered[0][0] > 0:
            nc.vector.memset(mixT[:, 24:32, 0:covered[0][0]], 0.0)
        s.wait('dve', tin)
        s.wait('dve', th)
        dT = s.aT[:, 0:2 * 6 * T].rearrange("p (b j t) -> p b j t", b=2, j=6)
        for gi in range(4):
            w = 2 ** (gi + 1)
            buf = gi % 2
            s.wait('dve', s.aT_free[buf])
            for j in range(6):
                oc = gi * 6 + j
                cur = uext[:, oc, :]
                k = 0
                step = 1
                while step < w:
                    nxt = s.ptmp[:, k % 2, :W]
                    t = s.mark('dve', nc.vector.tensor_tensor(out=nxt[:, step:W], in0=cur[:, step:W], in1=cur[:, 0:W - step], op=ALU.add))
                    s.wait('dve', t)
                    cur = nxt
                    step *= 2
                    k += 1
                for si, (c0, n, hist) in enumerate(seqs):
                    b = bases[si]
                    t = s.mark('dve', nc.vector.scalar_tensor_tensor(out=dT[:, buf, j, c0:c0 + n], in0=cur[:, b + 16:b + 16 + n], scalar=1.0 / w,
                                                                     in1=uext[:, oc, b + 16:b + 16 + n], op0=ALU.mult, op1=ALU.subtract))
                if first_main:
                    b = bases[0]
                    s.wait('dve', t)
                    t = s.mark('dve', nc.vector.tensor_tensor(out=s.ptmp[:, 0, 0:16], in0=cur[:, b + 16:b + 32], in1=s.invc[:, gi * 16:(gi + 1) * 16], op=ALU.mult))
                    s.wait('dve', t)
                    t = s.mark('dve', nc.vector.tensor_tensor(out=dT[:, buf, j, 0:16], in0=s.ptmp[:, 0, 0:16], in1=uext[:, oc, b + 16:b + 32], op=ALU.subtract))
                    s.wait('dve', t)
            s.wait('pe', t)
            pwv = pool_w[gi].rearrange("(cc p) d -> p cc d", p=128)

            def evac_pw(dcl, ps, tok, gi=gi):
                s.wait('act', tok)
                ch = gi * 6 + dcl
                return s.mark('act', nc.scalar.activation(out=mixT[:, ch, :], in_=ps, func=AF.Copy, scale=s.psc[:, ch:ch + 1]))
            tk2 = s.proj(lambda dcl: pwv[:, :, dcl * 128:(dcl + 1) * 128], 6, lambda cc, buf=buf: dT[:, buf, cc, :], T, 6, evac_pw)
            s.aT_free[buf] = tk2
        tcar = None
        for si, (c0, n, hist) in enumerate(seqs):
            b = bases[si]
            if hist == 'carry':
                tcar = s.mark('dve', nc.vector.tensor_copy(out=s.carry[:, :, 1:16], in_=uext[:, :, b + 16 + n - 15:b + 16 + n]))
                if last_main:
                    s.wait('sp', tcar)
                    with nc.allow_non_contiguous_dma(reason="pool state out"):
                        s.dma(s.oq, 'sp', poolT_out_p.rearrange("(c p) j -> p c j", p=128), s.carry[:, :, 1:16])
            elif hist is None:
                tcar = s.mark('dve', nc.vector.tensor_scalar(out=s.carry[:, :, 1:16], in0=uext[:, :, b + 16:b + 31], scalar1=s.halo_on[:, 0:1], scalar2=None, op0=ALU.mult))
            else:
                s.wait('sp', tin)
                with nc.allow_non_contiguous_dma(reason="pool state out"):
                    tcar = s.dma(s.oq, 'sp', poolT_out_s[:, hist, :].rearrange("(c p) j -> p c j", p=128), uext[:, :, b + 17:b + 32])
        act_last = s.bankfree[0] if s.bankfree[0][1] > s.bankfree[1][1] else s.bankfree[1]
        s.wait('pe', act_last)
        s.wait('pe', tm)
        wov = w_out.rearrange("(kc p) f -> p kc f", p=128)
        acc = s.acc3(T)

        def evac_out(oc, ps, tok):
            s.wait('dve', tok)
            s.wait('dve', tcar)
            return s.mark('dve', nc.vector.tensor_copy(out=acc[:, oc, :], in_=ps))
        tk3 = s.proj(lambda oc: wov[:, :, oc * 128:(oc + 1) * 128], 32, lambda kc: mixT[:, kc, :], T, 32, evac_out)
        s.hT_free = tk3
        s.acc_ready = s.bankfree[1] if s.bankfree[1][1] > s.bankfree[0][1] else s.bankfree[0]
        s.postnorm(T, 3, False, xsrc, dsts)

    def kvproj(s, T, c0, wkv, kT_out, vT_out, lfT_out):
        nc = s.nc
        s.prenorm(T, 6)
        hT = s.hT3(T)
        wv = wkv.rearrange("(kc p) f -> p kc f", p=128)

        def evac(oc, ps, tok):
            b = oc % 2
            s.wait('act', tok)
            s.wait('act', s.tmp_free[b])
            t = s.mark('act', nc.scalar.copy(out=s.tmp32[:, b, :T], in_=ps))
            s.wait('sp', t)
            dst = (kT_out if oc < 24 else vT_out)[(oc % 24) * 128:(oc % 24 + 1) * 128, c0:c0 + T]
            s.tmp_free[b] = s.dma(s.sq[b], 'sp', dst, s.tmp32[:, b, :T])
            return t
        s.proj(lambda oc: wv[:, :, oc * 128:(oc + 1) * 128], 32, lambda kc: hT[:, kc, :], T, 48, evac)
        s.wait('pe', s.bankfree[6])
        for kc in range(32):
            ins = s.mm(s.ps[6][0:24, :T], s.wf_sb[:, kc, :], hT[:, kc, :], kc == 0, kc == 31)
        t = s.mark('pe', ins)
        s.hT_free = t
        s.wait('act', t)
        s.wait('act', s.lf_free)
        s.wait('act', s.sg_free[0])
        s.wait('act', s.sg_free[1])
        z = s.lf[0:24, 0, :T]
        ab = s.lf[0:24, 1, :T]
        t = s.mark('act', nc.scalar.activation(out=z, in_=s.ps[6][0:24, :T], func=AF.Identity, bias=s.bf_sb[0:24, 0:1]))
        s.bankfree[6] = t
        s.wait('act', t)
        t = s.mark('act', nc.scalar.activation(out=ab, in_=z, func=AF.Abs))
        s.wait('act', t)
        t = s.mark('act', nc.scalar.activation(out=ab, in_=ab, func=AF.Exp, scale=-1.0))
        s.wait('act', t)
        t = s.mark('act', nc.scalar.activation(out=ab, in_=ab, func=AF.Ln, bias=s.onecol[0:24, 0:1]))
        s.wait('dve', t)
        t = s.mark('dve', nc.vector.scalar_tensor_tensor(out=z, in0=z, scalar=0.0, in1=ab, op0=ALU.min, op1=ALU.subtract))
        s.wait('sp', t)
        s.lf_free = s.dma(s.oq, 'sp', lfT_out[:, c0:c0 + T], z)


    def fox(s, Tq, qh, past, diag, out_ap, KB2=289.0, drow=None):
        nc = s.nc
        scale = 128.0 ** -0.5
        s.wait('act', s.pT_free[0])
        t = s.mark('act', nc.scalar.activation(out=s.pT[:, 0, :Tq], in_=qh, func=AF.Square))
        s.wait('pe', t)
        s.wait('pe', s.bankfree[6])
        t = s.mark('pe', s.mm(s.ps[6][0:1, :Tq], s.ones[:, 0:1], s.pT[:, 0, :Tq], True, True))
        s.pT_free[0] = t
        s.wait('act', t)
        s.wait('act', s.mrow_free)
        if drow is None:
            t = s.mark('act', nc.scalar.activation(out=s.mrow[0:1, :Tq], in_=s.ps[6][0:1, :Tq], func=AF.Sqrt, scale=KB2))
            s.bankfree[6] = t
        else:
            s.wait('act', s.m32_free)
            s.wait('act', s.sg_free[0])
            t = s.mark('act', nc.scalar.activation(out=s.mrow32[0:1, :Tq], in_=s.ps[6][0:1, :Tq], func=AF.Sqrt, scale=KB2))
            s.wait('pe', t)
            for r in range(4):
                tp = s.mark('pe', s.mm(s.ps[6][0:1, r * 128:(r + 1) * 128], s.dbias_bf[:, drow + r:drow + r + 1], s.ident[:, :], True, True))
            s.wait('dve', tp)
            t = s.mark('dve', nc.vector.scalar_tensor_tensor(out=s.mrow[0:1, :Tq], in0=s.ps[6][0:1, :Tq], scalar=1.0 / scale,
                                                             in1=s.mrow32[0:1, :Tq], op0=ALU.mult, op1=ALU.add))
            s.bankfree[6] = t
            s.m32_free = t
            s.sg_free[0] = t
        s.wait('pe', t)
        items = [('d', d) for d in diag] + [('p', kt) for kt in range(past[3])]
        n = len(items)
        s.wait('pe', s.bankfree[2])
        s.wait('pe', s.bankfree[4])
        for idx, (kind, it) in enumerate(items):
            b = idx % 2
            pb = idx % 3
            if kind == 'd':
                kTd, Vd, biasd, m, q0, mask = it
            else:
                kt = it
                kTd = past[0][:, kt * 128:(kt + 1) * 128]
                Vd = past[1][:, kt, :]
                biasd = past[2][:, kt:kt + 1]
                m, q0, mask = 128, 0, None
            s.wait('pe', s.bankfree[b])
            s.mm(s.ps[b][0:m, q0:Tq], kTd, qh[:, q0:Tq], True, False)
            t = s.mark('pe', s.mm(s.ps[b][0:m, q0:Tq], s.negones[0:1, 0:m], s.mrow[0:1, q0:Tq], False, True))
            s.wait('act', t)
            s.wait('act', s.pT_free[pb])
            te = s.mark('act', nc.scalar.activation(out=s.pT[0:m, pb, q0:Tq], in_=s.ps[b][0:m, q0:Tq], func=AF.Exp, scale=scale, bias=biasd))
            s.bankfree[b] = te
            if mask is not None:
                s.wait('dve', te)
                te = s.mark('dve', nc.vector.tensor_tensor(out=s.pT[0:m, pb, q0:q0 + m], in0=s.pT[0:m, pb, q0:q0 + m], in1=mask, op=ALU.mult))
            s.wait('pe', te)
            s.mm(s.ps[2][:, q0:Tq], Vd, s.pT[0:m, pb, q0:Tq], idx == 0, idx == n - 1)
            t = s.mark('pe', s.mm(s.ps[4][:, q0:Tq], s.ones[0:m, :], s.pT[0:m, pb, q0:Tq], idx == 0, idx == n - 1))
            s.pT_free[pb] = t
        s.mrow_free = t
        s.wait('dve', t)
        s.wait('dve', s.rstd_free)
        t1 = s.mark('dve', nc.vector.reciprocal(out=s.rl[:, :Tq], in_=s.ps[4][:, :Tq]))
        s.bankfree[4] = t1
        s.wait('dve', t1)
        t2 = s.mark('dve', nc.vector.tensor_tensor(out=out_ap, in0=s.ps[2][:, :Tq], in1=s.rl[:, :Tq], op=ALU.mult))
        s.bankfree[2] = t2
        s.rstd_free = t2
        return t, t2

    def bias_tables(s, lf_src, vis_src, nk, dst, wtri_dst=None):
        nc = s.nc
        N = 24 * nk
        A = lambda k: s.acc[:, k * 3072:k * 3072 + N]
        lfb = s.hT[:, 0:N]
        s.wait('sp', s.acc_free)
        t = s.dma(s.xq[0], 'sp', A(0), lf_src)
        if vis_src is not None:
            t = s.dma(s.xq[0], 'sp', A(3), vis_src)
        s.wait('dve', t)
        s.wait('dve', s.hT_free)
        t = s.mark('dve', nc.vector.tensor_copy(out=lfb, in_=A(0)))
        s.wait('dve', t)
        lfl = s.hT[:, 4096:4096 + N]
        t = s.mark('dve', nc.vector.tensor_tensor(out=lfl, in0=A(0), in1=lfb, op=ALU.subtract))
        s.wait('pe', t)
        tw = None
        for cch in range((N + 511) // 512):
            c0 = cch * 512
            c1 = min(N, c0 + 512)
            for (bk, lhs, dk) in ((0, s.triu, 1), (1, s.ones, 2)):
                s.wait('pe', s.bankfree[bk])
                s.mm(s.ps[bk][:, 0:c1 - c0], lhs[:, :], lfb[:, c0:c1], True, False)
                tm = s.mark('pe', s.mm(s.ps[bk][:, 0:c1 - c0], lhs[:, :], lfl[:, c0:c1], False, True))
                s.wait('dve', tm)
                tw = s.mark('dve', nc.vector.tensor_copy(out=A(dk)[:, c0:c1], in_=s.ps[bk][:, 0:c1 - c0]))
                s.bankfree[bk] = tw
        s.hT_free = tm
        s.wait('dve', tw)
        cur = A(2)
        if vis_src is not None:
            t = s.mark('dve', nc.vector.tensor_tensor(out=A(4), in0=A(2), in1=A(3), op=ALU.mult))
            s.wait('dve', t)
            cur = A(4)
        step = 1
        k = 0
        bufs = [A(2) if vis_src is not None else A(4), A(0)]
        while step < nk:
            nxt = bufs[k % 2]
            c3 = cur.rearrange("p (h k) -> p h k", k=nk)
            n3 = nxt.rearrange("p (h k) -> p h k", k=nk)
            nc.vector.tensor_tensor(out=n3[:, :, 0:nk - step], in0=c3[:, :, 0:nk - step], in1=c3[:, :, step:nk], op=ALU.add)
            t = s.mark('dve', nc.vector.tensor_copy(out=n3[:, :, nk - step:nk], in_=c3[:, :, nk - step:nk]))
            s.wait('dve', t)
            cur = nxt
            step *= 2
            k += 1
        res = A(1)
        t = s.mark('dve', nc.vector.tensor_tensor(out=res, in0=cur, in1=A(1), op=ALU.subtract))
        s.wait('dve', t)
        if vis_src is not None:
            t = s.mark('dve', nc.vector.tensor_scalar(out=A(3), in0=A(3), scalar1=BIG, scalar2=-BIG, op0=ALU.mult, op1=ALU.add))
            s.wait('dve', t)
            t = s.mark('dve', nc.vector.tensor_tensor(out=res, in0=res, in1=A(3), op=ALU.add))
        s.wait('sp', t)
        s.acc_free = s.dma(s.stq, 'sp', dst, res)
        s.acc_ready = s.acc_free

    def diag_bias(s, lf_src, rows, nk, group, dst_sb):
        nc = s.nc
        N = 24 * nk
        A = lambda k: s.acc[0:rows, k * 3072:k * 3072 + N]
        lfb = s.hT[0:rows, 0:N]
        s.wait('sp', s.acc_free)
        t = s.dma(s.xq[0], 'sp', A(0), lf_src)
        s.wait('dve', t)
        s.wait('dve', s.hT_free)
        t = s.mark('dve', nc.vector.tensor_copy(out=lfb, in_=A(0)))
        s.wait('dve', t)
        lfl = s.hT[0:rows, 4096:4096 + N]
        t = s.mark('dve', nc.vector.tensor_tensor(out=lfl, in0=A(0), in1=lfb, op=ALU.subtract))
        s.wait('pe', t)
        for (bk, lhs, dk) in ((0, s.triu, 1), (1, s.ones, 2)):
            s.wait('pe', s.bankfree[bk])
            s.mm(s.ps[bk][0:rows, 0:N], lhs[0:rows, 0:rows], lfb, True, False)
            tm = s.mark('pe', s.mm(s.ps[bk][0:rows, 0:N], lhs[0:rows, 0:rows], lfl, False, True))
            s.wait('dve', tm)
            tw = s.mark('dve', nc.vector.tensor_copy(out=A(dk), in_=s.ps[bk][0:rows, 0:N]))
            s.bankfree[bk] = tw
        s.hT_free = tm
        s.wait('dve', tw)
        W3 = A(1).rearrange("p (h k) -> p h k", k=nk)
        T3 = A(2).rearrange("p (h k) -> p h k", k=nk)
        d3 = dst_sb
        t = None
        for kt in range(nk):
            g0 = (kt // group) * group
            t = s.mark('dve', nc.vector.tensor_scalar(out=d3[0:rows, :, kt:kt + 1], in0=W3[:, :, kt:kt + 1], scalar1=-1.0, scalar2=None, op0=ALU.mult))
            for k2 in range(g0, kt):
                s.wait('dve', t)
                t = s.mark('dve', nc.vector.tensor_tensor(out=d3[0:rows, :, kt:kt + 1], in0=d3[0:rows, :, kt:kt + 1], in1=T3[:, :, k2:k2 + 1], op=ALU.subtract))
        s.wait('dve', t)
        for e in ('act', 'pe', 'sp'):
            s.wait(e, t)
        s.acc_free = t
        s.acc_ready = t

    def mixer_b(s, T, is_e, j, w_in, w_out, xsrc, dsts, B):
        nc = s.nc
        s.prenorm(T, 2)
        hT = s.hT3(T)
        wv = w_in.rearrange("(kc p) f -> p kc f", p=128)
        qs = B["qs"]

        def evac_in(oc, ps, tok):
            s.wait('act', tok)
            if oc < 24:
                b = oc % 2
                s.wait('act', s.tmp_free[b])
                t = s.mark('act', nc.scalar.copy(out=s.tmp32[:, b, :T], in_=ps))
                s.wait('sp', t)
                s.tmp_free[b] = s.dma(s.sq[b], 'sp', qs[oc * 128:(oc + 1) * 128, 0:T], s.tmp32[:, b, :T])
                return t
            return s.mark('act', nc.scalar.copy(out=s.qm[:, oc - 24, :T], in_=ps))
        s.wait('act', s.qm_free)
        tk = s.proj(lambda oc: wv[:, :, oc * 128:(oc + 1) * 128], 32, lambda kc: hT[:, kc, :], T, 32, evac_in)
        s.hT_free = tk
        mixT = s.hT3(T)
        s.wait('dve', tk)
        s.wait('act', tk)
        tlast = s.bankfree[1] if s.bankfree[1][1] > s.bankfree[0][1] else s.bankfree[0]
        if is_e:
            memsets = [((0, 16), s.smk[0], s.smv[0], s.kmx[:, 4:8]), ((16, 16), s.smk[1], s.smv[1], s.kmx[:, 8:12])]
        else:
            memsets = [((0, T), s.mkT, s.mV, s.kmx[:, 0:4])]
        s.wait('act', tlast)
        s.wait('pe', tlast)
        tm = None
        for (cols, kT, V, kmx) in memsets:
            c0, n = cols
            tm = s.mem_attn(T, s.qm, cols, kT, V, lambda h, i, c0=c0, n=n: mixT[:, 24 + 2 * h + i, c0:c0 + n], kmx)
        s.qm_free = tm
        accb = s.acc[:].bitcast(BF16)
        kbuf = accb[:, 0:16384]
        vbuf = accb[:, 16384:32768].rearrange("p (k f) -> p k f", f=128)
        sets = ([(0, 16, 0), (16, 16, 1)] if is_e else [(0, T, None)])
        for h in range(NH):
            for q in (s.sq[0], s.sq[1]):
                s.wait('pool', (q, q[1]))
            s.wait('pool', s.qh_free)
            tq = s.dma(s.kvq, 'pool', s.qh[:, :T], qs[h * 128:(h + 1) * 128, 0:T])
            for (c0, n, sq_) in sets:
                if sq_ is None:
                    nk = 128
                    kT_src = B["kT_all"][h * 128:(h + 1) * 128, :]
                    v_src = B["v_all"].rearrange("(k p) f -> p k f", p=128)[:, :, h * 128:(h + 1) * 128]
                    b_src = B["biasD"][j][:, h * 128:(h + 1) * 128]
                    kd_src = B["kT_own"][h * 128:(h + 1) * 128, j * 512:(j + 1) * 512]
                    vd_src = B["v_own"].rearrange("(k p) f -> p k f", p=128)[:, 4 * j:4 * j + 4, h * 128:(h + 1) * 128]
                else:
                    nk = 16
                    kT_src = B["kTc"][sq_][h * 128:(h + 1) * 128, :]
                    v_src = B["vc"][sq_].rearrange("(k p) f -> p k f", p=128)[:, :, h * 128:(h + 1) * 128]
                    b_src = B["biasC"][sq_][:, h * 16:(h + 1) * 16]
                    kd_src = B["kTn"][sq_][h * 128:(h + 1) * 128, :]
                    vd_src = B["vn"][sq_][:, h * 128:(h + 1) * 128]
                s.wait('pool', s.kv_free)
                s.wait('pool', s.acc_free)
                s.wait('sp', s.kv_free)
                s.dma(s.kvq, 'pool', kbuf[:, 0:nk * 128], kT_src)
                with nc.allow_non_contiguous_dma(reason="per-head V slices"):
                    hk = nk // 2
                    s.dma(s.kvq, 'pool', vbuf[:, 0:hk, :], v_src[:, 0:hk, :])
                    s.dma(s.kvq, 'pool', vbuf[:, hk:nk, :], v_src[:, hk:nk, :])
                    if sq_ is None:
                        s.dma(s.kvq, 'pool', s.kd[:, 0:512], kd_src)
                        s.dma(s.kvq, 'pool', s.vd[:, 0:4, :], vd_src)
                    else:
                        s.dma(s.kvq, 'pool', s.kd[:, 0:16], kd_src)
                        s.dma(s.kvq, 'pool', s.vd[0:16, 0, :], vd_src)
                tkv = s.dma(s.kvq, 'pool', s.bias_sb[:, 0:nk], b_src)
                for e in ('pe', 'act', 'dve'):
                    s.wait(e, tkv)
                if sq_ is None:
                    diag = [(s.kd[:, r * 128:(r + 1) * 128], s.vd[:, r, :], s.dbias[:, h * 16 + 4 * j + r: h * 16 + 4 * j + r + 1], 128, 128 * r, s.tri[:, :]) for r in range(4)]
                else:
                    diag = [(s.kd[:, 0:16], s.vd[0:16, 0, :], s.dbias_s[0:16, sq_ * 24 + h: sq_ * 24 + h + 1], 16, 0, s.tri[0:16, 0:16])]
                tpe, tdv = s.fox(n, s.qh[:, c0:c0 + n], (kbuf, vbuf, s.bias_sb, nk), diag, mixT[:, h, c0:c0 + n], drow=(h * 16 + 4 * j if sq_ is None else None))
                s.kv_free = tpe
            s.qh_free = tpe
        s.acc_free = tpe
        wov = w_out.rearrange("(kc p) f -> p kc f", p=128)
        acc = s.acc3(T)
        s.wait('pe', tdv)
        s.wait('pe', tm)

        def evac_out(oc, ps, tok):
            s.wait('dve', tok)
            return s.mark('dve', nc.vector.tensor_copy(out=acc[:, oc, :], in_=ps))
        tk3 = s.proj(lambda oc: wov[:, :, oc * 128:(oc + 1) * 128], 32, lambda kc: mixT[:, kc, :], T, 32, evac_out)
        s.hT_free = tk3
        s.acc_ready = s.bankfree[1] if s.bankfree[1][1] > s.bankfree[0][1] else s.bankfree[0]
        s.acc_free = None
        s.postnorm(T, 3, False, xsrc, dsts)


def build_A(nf=NF, tiles=None, parts=("ffn1", "mix", "ffn2", "kv")):
    global R
    R = 5
    c = Ctx(nf)
    nc = c.nc
    TT = TE_A + TP
    dt = lambda name, shape, kind="ExternalInput": nc.dram_tensor(name, shape, F32, kind=kind).ap()
    xT = dt("xT", [D, TT])
    gains = dt("gains", [128, 7 * 32])
    gmem = dt("gmem", [128, 7 * 32])
    wg1 = dt("wg1", [D, DFF]); wu1 = dt("wu1", [D, DFF]); wd1 = dt("wd1", [DFF, D])
    wg2 = dt("wg2", [D, DFF]); wu2 = dt("wu2", [D, DFF]); wd2 = dt("wd2", [DFF, D])
    w_in = dt("w_in", [D, D]); pool_w = dt("pool_w", [4, 768, 768]); w_out = dt("w_out", [D, D])
    psc_in = dt("psc", [128, 24])
    wkv = dt("wkv", [D, 2 * DPOOL]); wf = dt("wf", [D, NH]); bfv = dt("bf", [NH, 1])
    memT = dt("memT", [D, 256]); wmem0 = dt("wmem0", [D, 2048]); wmem1 = dt("wmem1", [D, 2048])
    cmkT = dt("cmkT", [2, 1024, 256]); cmv = dt("cmv", [2, 256, 1024])
    poolT_hist = dt("poolT_hist", [DPOOL, 2, 15])
    meta = dt("meta", [128, 80])
    xo = dt("xo", [D, TT], "ExternalOutput")
    kT_out = dt("kT_out", [DPOOL, TT], "ExternalOutput"); vT_out = dt("vT_out", [DPOOL, TT], "ExternalOutput")
    lfT_out = dt("lfT_out", [NH, TT], "ExternalOutput")
    memk0 = dt("memk0", [1024, 256], "ExternalOutput"); memv0 = dt("memv0", [1024, 256], "ExternalOutput")
    memk1 = dt("memk1", [1024, 256], "ExternalOutput"); memv1 = dt("memv1", [1024, 256], "ExternalOutput")
    poolT_out_s = dt("poolT_out_s", [DPOOL, 2, 15], "ExternalOutput"); poolT_out_p = dt("poolT_out_p", [DPOOL, 15], "ExternalOutput")
    xs1 = nc.dram_tensor("xs1", [D, TT], F32).ap()
    xs2 = nc.dram_tensor("xs2", [D, TT], F32).ap()
    c.mkT = c.sb("mkT", [128, 8, 256], BF16); c.mV = c.sb("mV", [128, 2, 1024], BF16)
    c.qm = c.sb("qm", [128, 8, 512], BF16)
    c.pT = c.sb("pT", [128, 3, 512], BF16)
    c.mrow = c.sb("mrow", [1, 512], BF16)
    c.rl = c.rstd
    c.ptmp = c.sb("ptmp", [128, 2, 544], F32)
    c.carry = c.sb("carry", [128, 24, 16], F32)
    c.psc = c.sb("pscs", [128, 24], F32)
    c.metas = c.sb("metas", [128, 80], F32)
    c.kmx = c.sb("kmx", [128, 12], F32)
    c.ident = c.sb("ident", [128, 128], BF16)
    c.wf_sb = c.sb("wf_sb", [128, 32, NH], BF16)
    c.bf_sb = c.sb("bf_sb", [NH, 1], F32)
    c.lf = c.sg
    c.psb = c.st.enter_context(nc.psum_tensor("psb", [128, 2048], BF16)) if False else None
    c.pT_free = [None, None, None]; c.mrow_free = None; c.rl_free = None; c.qm_free = None; c.lf_free = None
    c.invc = c.metas[:, 0:64]; c.halo_on = c.metas[:, 64:65]
    c.consts(gmem)
    t = c.dma(c.cq, 'sp', c.psc[:], psc_in)
    t = c.dma(c.cq, 'sp', c.metas[:], meta)
    t = c.dma(c.cq, 'sp', c.bf_sb[:], bfv)
    with nc.allow_non_contiguous_dma(reason="small w_f"):
        t2 = c.dma(c.cq, 'pool', c.wf_sb[:], wf.rearrange("(kc p) h -> p kc h", p=128))
    for e in ('pe', 'act', 'dve'):
        c.wait(e, t2)
    ti = c.mark('pool', nc.gpsimd.affine_select(out=c.ident[:], in_=c.ones[:], pattern=[[-1, 128]], compare_op=ALU.is_equal, fill=0.0, base=0, channel_multiplier=1))
    c.wait('pe', ti)
    smb = c.acc[:, 12288:16384].bitcast(BF16)
    c.smk = [smb[:, i * 2048:(i + 1) * 2048].rearrange("p (c s) -> p c s", s=256) for i in range(2)]
    c.smv = [smb[:, 4096 + i * 2048:4096 + (i + 1) * 2048].rearrange("p (k f) -> p k f", f=1024) for i in range(2)]
    for i in range(2):
        c.dma(c.cq, 'pool', c.smk[i], cmkT[i].rearrange("(c p) s -> p c s", p=128))
        t2 = c.dma(c.cq, 'pool', c.smv[i], cmv[i].rearrange("(k p) f -> p k f", p=128))
    for e in ('pe', 'act', 'dve'):
        c.wait(e, t2)
    c.psb = c.ps[7].bitcast(BF16) if hasattr(c.ps[7], "bitcast") else None
    c.setup_mem(memT, wmem1, 1, memk1, memv1)
    c.setup_mem(memT, wmem0, 0, memk0, memv0)
    c.kmax(c.mkT, c.kmx[:, 0:4])
    c.kmax(c.smk[0], c.kmx[:, 4:8])
    c.kmax(c.smk[1], c.kmx[:, 8:12])
    t = c.dma(c.cq, 'sp', c.gv[:], gains)
    c.wait('dve', t)
    tk = c.mark('dve', nc.vector.tensor_scalar(out=c.gvh[:], in0=c.gv[:], scalar1=0.5, scalar2=None, op0=ALU.mult))
    for e in ('pe', 'act', 'dve'):
        c.wait(e, tk)
    if tiles is None:
        tiles = [(0, TE_A)] + [(TE_A + 512 * i, 512) for i in range(4)]
    for ti_, (c0, T) in enumerate(tiles):
        is_e = (c0 == 0)
        cs = slice(c0, c0 + T)
        c.load_x(T, xT[:, cs])
        if "ffn1" in parts:
            c.ffn(T, wg1, wu1, wd1, 0, 1, xT[:, cs], [xs1[:, cs]])
        if "mix" in parts:
            if is_e:
                seqs = [(0, 15, None), (15, 16, 0), (31, 16, 1)]
                memsets = [((15, 16), c.smk[0], c.smv[0], c.kmx[:, 4:8]), ((31, 16), c.smk[1], c.smv[1], c.kmx[:, 8:12])]
            else:
                seqs = [(0, T, 'carry')]
                memsets = [((0, T), c.mkT, c.mV, c.kmx[:, 0:4])]
            c.mixer_a(T, seqs, memsets, w_in, pool_w, w_out, xs1[:, cs], [xs2[:, cs]], first_main=(ti_ == 1),
                      poolT_hist=poolT_hist, poolT_out_s=poolT_out_s, poolT_out_p=poolT_out_p, last_main=(ti_ == len(tiles) - 1))
        if "ffn2" in parts:
            c.ffn(T, wg2, wu2, wd2, 4, 5, xs2[:, cs], [xs1[:, cs], xo[:, cs]])
        if "kv" in parts:
            c.kvproj(T, c0, wkv, kT_out, vT_out, lfT_out)
    for q in (c.stq, c.oq, c.sq[0], c.sq[1], c.cq):
        c.wait('sp', (q, q[1]))
    c.st.close()
    return nc


def build_B(nf=NF, tiles=None):
    global R
    R = 4
    c = Ctx(nf)
    nc = c.nc
    TT = TE_B + TP
    dt = lambda name, shape, kind="ExternalInput": nc.dram_tensor(name, shape, F32, kind=kind).ap()
    xT = dt("xT", [D, TT])
    gains = dt("gains", [128, 7 * 32])
    wg1 = dt("wg1", [D, DFF]); wu1 = dt("wu1", [D, DFF]); wd1 = dt("wd1", [DFF, D])
    wg2 = dt("wg2", [D, DFF]); wu2 = dt("wu2", [D, DFF]); wd2 = dt("wd2", [DFF, D])
    w_in = dt("w_in", [D, D]); w_out = dt("w_out", [D, D])
    memT = dt("memT", [D, 256]); wmem1 = dt("wmem1", [D, 2048])
    cmkT = dt("cmkT", [2, 1024, 256]); cmv = dt("cmv", [2, 256, 1024])
    B = {}
    B["kT_all"] = dt("kT_all", [DPOOL, 16384]); B["v_all"] = dt("v_all", [16384, DPOOL])
    lfk = dt("lfk", [128, 3072]); visx = dt("visx", [4, 128, 3072])
    B["kT_own"] = dt("kT_own", [DPOOL, TP]); B["v_own"] = dt("v_own", [TP, DPOOL]); lfo = dt("lfo", [128, 24 * 16])
    B["kTc"] = dt("kTc", [2, DPOOL, PAST]); B["vc"] = dt("vc", [2, PAST, DPOOL]); lfc = dt("lfc", [2, 128, 24 * 16])
    B["kTn"] = dt("kTn", [2, DPOOL, 16]); B["vn"] = dt("vn", [2, 16, DPOOL]); lfn = dt("lfn", [2, 16, 24])
    yo = dt("yo", [D, TT], "ExternalOutput")
    dmk = dt("dmk", [1024, 256], "ExternalOutput"); dmv = dt("dmv", [1024, 256], "ExternalOutput")
    xs1 = nc.dram_tensor("xs1", [D, TT], F32).ap()
    xs2 = nc.dram_tensor("xs2", [D, TT], F32).ap()
    B["qs"] = dt("qs", [DPOOL, 512], "ExternalOutput")
    B["biasD"] = [dt(f"biasD{j}", [128, 3072], "ExternalOutput") for j in range(4)]
    B["biasC"] = [dt(f"biasC{j}", [128, 24 * 16], "ExternalOutput") for j in range(2)]
    c.mkT = c.sb("mkT", [128, 8, 256], BF16); c.mV = c.sb("mV", [128, 2, 1024], BF16)
    c.qm = c.sb("qm", [128, 8, 512], BF16)
    c.pT = c.sb("pT", [128, 3, 512], BF16)
    c.mrow = c.sb("mrow", [1, 512], BF16)
    c.rl = c.rstd
    c.kmx = c.sb("kmx", [128, 12], F32)
    c.ident = c.sb("ident", [128, 128], BF16)
    c.triu = c.sb("triu", [128, 128], BF16)
    c.tri = c.triu
    c.qh = c.sb("qh", [128, 512], BF16)
    c.kd = c.sb("kd", [128, 512], BF16)
    c.vd = c.sb("vd", [128, 4, 128], BF16)
    c.bias_sb = c.sb("bias_sb", [128, 128], F32)
    c.dbias = c.sb("dbias", [128, 4 * 96], F32)
    c.dbias_s = c.sb("dbias_s", [128, 48], F32)
    c.dbias_bf = c.sb("dbias_bf", [128, 4 * 96], BF16)
    c.mrow32 = c.sg[:, 0, :]
    c.m32_free = None
    c.smk = [c.sb(f"smk{i}", [128, 8, 256], BF16) for i in range(2)]
    c.smv = [c.sb(f"smv{i}", [128, 2, 1024], BF16) for i in range(2)]
    c.kvq = c.sem("kvq")
    c.pT_free = [None, None, None]; c.mrow_free = None; c.qm_free = None; c.kv_free = None; c.qh_free = None
    c.consts(gains)
    ti = c.mark('pool', nc.gpsimd.affine_select(out=c.triu[:], in_=c.ones[:], pattern=[[1, 128]], compare_op=ALU.is_ge, fill=0.0, base=0, channel_multiplier=-1))
    ti = c.mark('pool', nc.gpsimd.affine_select(out=c.ident[:], in_=c.ones[:], pattern=[[-1, 128]], compare_op=ALU.is_equal, fill=0.0, base=0, channel_multiplier=1))
    for e in ('pe', 'dve', 'act'):
        c.wait(e, ti)
    for i in range(2):
        c.dma(c.cq, 'pool', c.smk[i][:], cmkT[i].rearrange("(c p) s -> p c s", p=128))
        t2 = c.dma(c.cq, 'pool', c.smv[i][:], cmv[i].rearrange("(k p) f -> p k f", p=128))
    for e in ('pe', 'act', 'dve'):
        c.wait(e, t2)
    c.psb = c.ps[7].bitcast(BF16)
    for j in range(4):
        c.bias_tables(lfk, visx[j], 128, B["biasD"][j])
    for i in range(2):
        c.bias_tables(lfc[i], None, 16, B["biasC"][i])
    c.diag_bias(lfo, 128, 16, 4, c.dbias[:].rearrange("p (h k) -> p h k", k=16))
    for i in range(2):
        c.diag_bias(lfn[i], 16, 1, 1, c.dbias_s[:, i * 24:(i + 1) * 24].rearrange("p (h k) -> p h k", k=1))
    tdb = c.mark('dve', nc.vector.tensor_copy(out=c.dbias_bf[:], in_=c.dbias[:]))
    c.wait('pe', tdb)
    dbg1 = dt("dbg_dbias", [128, 384], "ExternalOutput"); dbg2 = dt("dbg_dbias_s", [128, 48], "ExternalOutput")
    c.dma(c.oq, 'sp', dbg1, c.dbias[:]); c.dma(c.oq, 'sp', dbg2, c.dbias_s[:])
    c.setup_mem(memT, wmem1, 6, dmk, dmv)
    c.kmax(c.mkT, c.kmx[:, 0:4])
    c.kmax(c.smk[0], c.kmx[:, 4:8])
    c.kmax(c.smk[1], c.kmx[:, 8:12])
    if tiles is None:
        tiles = [(0, TE_B)] + [(TE_B + 512 * i, 512) for i in range(4)]
    for ti_, (c0, T) in enumerate(tiles):
        is_e = (c0 == 0)
        cs = slice(c0, c0 + T)
        c.load_x(T, xT[:, cs])
        c.ffn(T, wg1, wu1, wd1, 0, 1, xT[:, cs], [xs1[:, cs]])
        c.mixer_b(T, is_e, ti_ - 1, w_in, w_out, xs1[:, cs], [xs2[:, cs]], B)
        c.ffn(T, wg2, wu2, wd2, 4, 5, xs2[:, cs], [yo[:, cs]])
    for q in (c.stq, c.oq, c.sq[0], c.sq[1], c.cq, c.kvq):
        c.wait('sp', (q, q[1]))
    c.st.close()
    return nc


def _gl(rows):
    g = np.zeros((7, D), np.float32)
    for k, r in enumerate(rows):
        if r is not None:
            g[k] = r
    return np.ascontiguousarray(g.reshape(7, 32, 128).transpose(2, 0, 1).reshape(128, 224))


def prep_A(inp, c):
    f = lambda a: np.ascontiguousarray(np.asarray(a, dtype=np.float32))
    xp = inp["x_prompt"][0]
    xs = inp["x_sample"]
    halo = np.zeros((15, D), np.float32) if c == 0 else xp[TP * c - 15:TP * c]
    rows = np.concatenate([halo, xs[2 * c], xs[2 * c + 1], xp[TP * c:TP * (c + 1)]], 0)
    ng = inp["norm_g"]
    meta = np.zeros((128, 80), np.float32)
    for gi, w in enumerate((2, 4, 8, 16)):
        for t in range(16):
            pos = TP * c + t
            meta[:, gi * 16 + t] = 1.0 / min(pos + 1, w)
    meta[:, 64] = 0.0 if c == 0 else 1.0
    cmk = inp["cache_mem_k"][0, 2 * c:2 * c + 2].reshape(2, 256, 1024)
    return {
        "xT": f(rows.T),
        "gains": _gl([ng[0, k] for k in range(6)] + [inp["g_kv"]]),
        "gmem": _gl([inp["g_mem"][0], inp["g_mem"][1]]),
        "wg1": f(inp["w_ffn_gate"][0, 0]), "wu1": f(inp["w_ffn_up"][0, 0]), "wd1": f(inp["w_ffn_down"][0, 0]),
        "wg2": f(inp["w_ffn_gate"][0, 1]), "wu2": f(inp["w_ffn_up"][0, 1]), "wd2": f(inp["w_ffn_down"][0, 1]),
        "w_in": f(inp["w_in_a"][0]), "pool_w": f(inp["pool_w"][0]), "w_out": f(inp["w_out_a"][0]),
        "psc": f(inp["pool_scale"][0].reshape(24, 128).T),
        "wkv": f(inp["w_kv"]), "wf": f(inp["w_f"]), "bf": f(inp["b_f"].reshape(NH, 1)),
        "memT": f(inp["mem_prompt"][0].T), "wmem0": f(inp["w_mem_kv"][0]), "wmem1": f(inp["w_mem_kv"][1]),
        "cmkT": f(cmk.transpose(0, 2, 1)), "cmv": f(inp["cache_mem_v"][0, 2 * c:2 * c + 2].reshape(2, 256, 1024)),
        "poolT_hist": f(inp["state_pool"][0, 2 * c:2 * c + 2].transpose(2, 0, 1)),
        "meta": meta,
    }


def _hk(lf, nk):
    return np.ascontiguousarray(lf.reshape(nk, 128, NH).transpose(1, 2, 0).reshape(128, NH * nk).astype(np.float32))


def prep_B(inp, resA, c, kT_all, v_all, lfk):
    f = lambda a: np.ascontiguousarray(np.asarray(a, dtype=np.float32))
    ng = inp["norm_g"]
    r = resA[c]
    visx = np.zeros((4, 128, NH, 128), np.float32)
    for j in range(4):
        visx[j, :, :, :16 * c + 4 * j] = 1.0
    cmk = inp["cache_mem_k"][1, 2 * c:2 * c + 2].reshape(2, 256, 1024)
    m = {
        "xT": f(np.concatenate([r["xo"][:, 15:TE_A], r["xo"][:, TE_A:]], 1)),
        "gains": _gl([ng[1, k] for k in range(6)] + [inp["g_mem"][1]]),
        "wg1": f(inp["w_ffn_gate"][1, 0]), "wu1": f(inp["w_ffn_up"][1, 0]), "wd1": f(inp["w_ffn_down"][1, 0]),
        "wg2": f(inp["w_ffn_gate"][1, 1]), "wu2": f(inp["w_ffn_up"][1, 1]), "wd2": f(inp["w_ffn_down"][1, 1]),
        "w_in": f(inp["w_in_b"][0]), "w_out": f(inp["w_out_b"][0]),
        "memT": f(inp["mem_prompt"][0].T), "wmem1": f(inp["w_mem_kv"][1]),
        "cmkT": f(cmk.transpose(0, 2, 1)), "cmv": f(inp["cache_mem_v"][1, 2 * c:2 * c + 2].reshape(2, 256, 1024)),
        "kT_all": kT_all, "v_all": v_all, "lfk": lfk, "visx": f(visx.reshape(4, 128, 3072)),
        "kT_own": f(r["kT_out"][:, TE_A:]), "v_own": f(r["vT_out"][:, TE_A:].T), "lfo": _hk(f(r["lfT_out"][:, TE_A:].T), 16),
        "kTc": f(inp["cache_fox_k"][2 * c:2 * c + 2].reshape(2, PAST, DPOOL).transpose(0, 2, 1)),
        "vc": f(inp["cache_fox_v"][2 * c:2 * c + 2].reshape(2, PAST, DPOOL)),
        "lfc": np.stack([_hk(f(inp["cache_fox_logf"][2 * c + s_]), 16) for s_ in range(2)]),
        "kTn": f(np.stack([r["kT_out"][:, 15 + 16 * s_:31 + 16 * s_] for s_ in range(2)])),
        "vn": f(np.stack([r["vT_out"][:, 15 + 16 * s_:31 + 16 * s_].T for s_ in range(2)])),
        "lfn": f(np.stack([r["lfT_out"][:, 15 + 16 * s_:31 + 16 * s_].T for s_ in range(2)])),
    }
    return m


def kernel(**inputs):
    inp = {k: np.asarray(v) for k, v in inputs.items()}
    f = lambda a: np.ascontiguousarray(np.asarray(a, dtype=np.float32))
    ncA = build_A()
    resA = run_bass_kernel_spmd(ncA, [prep_A(inp, c) for c in range(NCORES)], core_ids=list(range(NCORES))).results
    kT_all = f(np.concatenate([r["kT_out"][:, TE_A:] for r in resA], 1))
    v_all = f(np.concatenate([r["vT_out"][:, TE_A:].T for r in resA], 0))
    lf_all = f(np.concatenate([r["lfT_out"][:, TE_A:].T for r in resA], 0))
    lfk = _hk(lf_all, 128)
    ng = inp["norm_g"]
    mapsB = [prep_B(inp, resA, c, kT_all, v_all, lfk) for c in range(NCORES)]
    ncB = build_B()
    resB = run_bass_kernel_spmd(ncB, mapsB, core_ids=list(range(NCORES))).results
    S = 16384
    y_p = np.zeros((1, S, D), np.float32); y_s = np.zeros((16, 16, D), np.float32)
    k_p = np.zeros((1, S, NH, 128), np.float32); v_p = np.zeros((1, S, NH, 128), np.float32); lf_p = np.zeros((1, S, NH), np.float32)
    k_s = np.zeros((16, 16, NH, 128), np.float32); v_s = np.zeros((16, 16, NH, 128), np.float32); lf_s = np.zeros((16, 16, NH), np.float32)
    mk = np.zeros((2, 1, 256, 4, 256), np.float32); mv = np.zeros((2, 1, 256, 4, 256), np.float32)
    pool_p = np.zeros((1, 1, 15, DPOOL), np.float32); pool_s = np.zeros((1, 16, 15, DPOOL), np.float32)
    for c in range(NCORES):
        ra, rb = resA[c], resB[c]
        sl = slice(TP * c, TP * (c + 1))
        y_p[0, sl] = rb["yo"][:, TE_B:].T
        k_p[0, sl] = ra["kT_out"][:, TE_A:].T.reshape(TP, NH, 128)
        v_p[0, sl] = ra["vT_out"][:, TE_A:].T.reshape(TP, NH, 128)
        lf_p[0, sl] = ra["lfT_out"][:, TE_A:].T
        for s_ in range(2):
            b = 2 * c + s_
            y_s[b] = rb["yo"][:, 16 * s_:16 * s_ + 16].T
            k_s[b] = ra["kT_out"][:, 15 + 16 * s_:31 + 16 * s_].T.reshape(16, NH, 128)
            v_s[b] = ra["vT_out"][:, 15 + 16 * s_:31 + 16 * s_].T.reshape(16, NH, 128)
            lf_s[b] = ra["lfT_out"][:, 15 + 16 * s_:31 + 16 * s_].T
            pool_s[0, b] = ra["poolT_out_s"][:, s_, :].T
    for l in range(2):
        mk[l, 0] = resA[0]["memk%d" % l].T.reshape(256, 4, 256)
        mv[l, 0] = resA[0]["memv%d" % l].T.reshape(256, 4, 256)
    pool_p[0, 0] = resA[NCORES - 1]["poolT_out_p"].T
    return (y_p, y_s, k_p, v_p, lf_p, mk, mv, pool_p, k_s, v_s, lf_s, pool_s)
```

```python
import numpy as np
import concourse.bass as bass
import concourse.mybir as mybir
from concourse.bass_utils import run_bass_kernel_spmd
from contextlib import ExitStack

F32 = mybir.dt.float32
BF16 = mybir.dt.bfloat16
AF = mybir.ActivationFunctionType
ALU = mybir.AluOpType
AX = mybir.AxisListType

D = 4096
NKC = 32
DFF = 11008
NF = 86
GF = 8
DPOOL = 3072
NH = 24
EPS = 1e-6
R = 5
NCORES = 8
NPAST = [28, 60, 92, 124]
TP = 2048
TE_A = 47
TE_B = 32
PAST = 2048
BIG = 30000.0


def p3(ap2d):
    return ap2d.rearrange("(c p) t -> p c t", p=128)


class Ctx:
    def __init__(s, nf=NF):
        s.nc = bass.Bass("TRN2", target_bir_lowering=False)
        s.st = ExitStack()
        nc = s.nc
        s.NF = nf
        s.eng = {'pe': nc.tensor, 'act': nc.scalar, 'dve': nc.vector, 'pool': nc.gpsimd, 'sp': nc.sync}
        s.pg = {e: s.sem("pg_" + e) for e in ('pe', 'act', 'dve', 'pool')}
        s.wf = s.sem("wfree")
        s.waited = {}
        s.acc = s.sb("acc", [128, 16384], F32)
        s.hT = s.sb("hT", [128, 16384], BF16)
        s.aT = s.sb("aT", [128, 8192], BF16)
        s.ring = [s.sb(f"ring{i}", [128, 4096], BF16) for i in range(R)]
        s.rld = [s.sem(f"rld{i}") for i in range(R)]
        s.rn = 0
        s.xres = s.sb("xres", [128, 4, 512], F32)
        s.rstd = s.sb("rstd", [128, 512], F32)
        s.tmp32 = s.sb("tmp32", [128, 2, 512], F32)
        s.sg = s.sb("sg", [128, 2, 512], F32)
        s.gv = s.sb("gv", [128, 7 * 32], F32)
        s.gvh = s.sb("gvh", [128, 7 * 32], F32)
        s.ones = s.sb("ones", [128, 128], BF16)
        s.negones = s.sb("negones", [128, 128], BF16)
        s.onecol = s.sb("onecol", [128, 1], F32)
        s.ps = [s.st.enter_context(nc.psum_tensor(f"ps{i}", [128, 512], F32)) for i in range(8)]
        s.bankfree = [None] * 8
        s.xq = [s.sem("xq0"), s.sem("xq1")]
        s.stq = s.sem("stq")
        s.sq = [s.sem("sq0"), s.sem("sq1")]
        s.cq = s.sem("cq")
        s.oq = s.sem("oq")
        s.acc_ready = None
        s.hT_free = None
        s.hT_ready = None
        s.xres_free = [None, None]
        s.tmp_free = [None, None]
        s.sg_free = [None, None]
        s.aT_free = [None, None]
        s.rstd_free = None
        s.acc_free = None

    def sem(s, name):
        return [s.st.enter_context(s.nc.semaphore(name)), 0]

    def sb(s, name, shape, dt):
        return s.st.enter_context(s.nc.sbuf_tensor(name, shape, dt))

    def mark(s, e, ins):
        p = s.pg[e]
        p[1] += 1
        ins.then_inc(p[0], 1)
        return (p, p[1])

    def wait(s, cons, tok):
        if tok is None:
            return
        so, v = tok
        key = (cons, id(so))
        if s.waited.get(key, 0) >= v:
            return
        s.waited[key] = v
        s.eng[cons].wait_ge(so[0], v)

    def dma(s, q, eng, out, in_):
        ins = s.eng[eng].dma_start(out=out, in_=in_)
        q[1] += 16
        ins.then_inc(q[0], 16)
        return (q, q[1])

    def mm(s, out, lhsT, rhs, start, stop):
        return s.nc.tensor.matmul(out, lhsT=lhsT, rhs=rhs, start=start, stop=stop)

    def rget(s, src, k, w):
        n = s.rn
        s.rn += 1
        i = n % R
        if n >= R:
            s.wait('pool', (s.wf, n - R + 1))
        view = s.ring[i][:, 0:k * w].rearrange("p (k w) -> p k w", w=w)
        tok = s.dma(s.rld[i], 'pool', view, src)
        s.wait('pe', tok)
        return view

    def rrel(s, ins):
        s.wf[1] += 1
        ins.then_inc(s.wf[0], 1)
        return (s.wf, s.wf[1])

    def acc3(s, T):
        return s.acc[:, 0:32 * T].rearrange("p (c t) -> p c t", t=T)

    def hT3(s, T):
        return s.hT[:, 0:32 * T].rearrange("p (c t) -> p c t", t=T)

    def consts(s, gsrc):
        nc = s.nc
        t = s.dma(s.cq, 'sp', s.gv[:], gsrc)
        nc.vector.memset(s.ones[:], 1.0)
        nc.vector.memset(s.negones[:], -1.0)
        nc.vector.memset(s.onecol[:], 1.0)
        s.wait('dve', t)
        tk = s.mark('dve', nc.vector.tensor_scalar(out=s.gvh[:], in0=s.gv[:], scalar1=0.5, scalar2=None, op0=ALU.mult))
        for e in ('pe', 'act', 'dve', 'pool'):
            s.wait(e, tk)

    def stats(s, T, tsq):
        nc = s.nc
        hT = s.hT3(T)
        s.wait('pe', tsq)
        s.wait('pe', s.bankfree[6])
        for ch in range(32):
            ins = s.mm(s.ps[6][:, :T], s.ones[:, :], hT[:, ch, :], ch == 0, ch == 31)
        tss = s.mark('pe', ins)
        s.hT_free = tss
        s.wait('dve', tss)
        s.wait('dve', s.rstd_free)
        t1 = s.mark('dve', nc.vector.tensor_scalar(out=s.rstd[:, :T], in0=s.ps[6][:, :T], scalar1=1.0 / D, scalar2=EPS,
                                                   op0=ALU.mult, op1=ALU.add))
        s.bankfree[6] = t1
        s.wait('act', t1)
        t2 = s.mark('act', nc.scalar.activation(out=s.rstd[:, :T], in_=s.rstd[:, :T], func=AF.Sqrt))
        s.wait('dve', t2)
        t3 = s.mark('dve', nc.vector.reciprocal(out=s.rstd[:, :T], in_=s.rstd[:, :T]))
        s.wait('dve', t3)
        return t3

    def squares(s, T):
        nc = s.nc
        acc = s.acc3(T)
        hT = s.hT3(T)
        s.wait('act', s.acc_ready)
        s.wait('act', s.hT_free)
        for ch in range(32):
            ins = nc.scalar.activation(out=hT[:, ch, :], in_=acc[:, ch, :], func=AF.Square)
        return s.mark('act', ins)

    def prenorm(s, T, gi):
        nc = s.nc
        acc = s.acc3(T)
        hT = s.hT3(T)
        tsq = s.squares(T)
        s.stats(T, tsq)
        s.wait('dve', s.acc_ready)
        for ch in range(32):
            ins = nc.vector.scalar_tensor_tensor(out=hT[:, ch, :], in0=acc[:, ch, :],
                                                 scalar=s.gv[:, gi * 32 + ch: gi * 32 + ch + 1], in1=s.rstd[:, :T],
                                                 op0=ALU.mult, op1=ALU.mult)
        s.hT_ready = s.mark('dve', ins)
        s.rstd_free = s.hT_ready
        s.wait('pe', s.hT_ready)

    def postnorm(s, T, gi, half, xsrc, dsts):
        nc = s.nc
        acc = s.acc3(T)
        tsq = s.squares(T)
        s.stats(T, tsq)
        gt = s.gvh if half else s.gv
        s.wait('sp', (s.stq, s.stq[1]))
        t2 = None
        for pr in range(16):
            b = pr % 2
            s.wait('sp', s.xres_free[b])
            tl = s.dma(s.xq[b], 'sp', s.xres[:, 2 * b:2 * b + 2, :T], p3(xsrc[pr * 256:(pr + 1) * 256, :]))
            for k in range(2):
                ch = 2 * pr + k
                s.wait('dve', tl)
                s.wait('dve', s.tmp_free[k])
                t1 = s.mark('dve', nc.vector.scalar_tensor_tensor(out=s.tmp32[:, k, :T], in0=acc[:, ch, :],
                                                                  scalar=gt[:, gi * 32 + ch: gi * 32 + ch + 1],
                                                                  in1=s.rstd[:, :T], op0=ALU.mult, op1=ALU.mult))
                s.wait('dve', t1)
                t2 = s.mark('dve', nc.vector.tensor_tensor(out=acc[:, ch, :], in0=s.tmp32[:, k, :T],
                                                           in1=s.xres[:, 2 * b + k, :T], op=ALU.add))
            s.xres_free[b] = t2
        s.acc_ready = t2
        s.rstd_free = t2
        s.wait('sp', t2)
        for dst in dsts:
            s.dma(s.stq, 'sp', p3(dst), acc)
        s.acc_free = (s.stq, s.stq[1])

    def load_x(s, T, src):
        s.wait('sp', s.acc_free)
        s.wait('sp', s.hT_ready)
        s.wait('sp', s.acc_ready)
        s.acc_ready = s.dma(s.xq[0], 'sp', s.acc3(T), p3(src))

    def proj(s, Wfn, n_k, rhs, T, n_oc, evac, banks=(0, 1)):
        for oc in range(n_oc):
            b = banks[oc % len(banks)]
            w = s.rget(Wfn(oc), n_k, 128)
            s.wait('pe', s.bankfree[b])
            for kc in range(n_k):
                ins = s.mm(s.ps[b][:, :T], w[:, kc, :], rhs(kc), kc == 0, kc == n_k - 1)
            tok = s.rrel(ins)
            s.bankfree[b] = evac(oc, s.ps[b][:, :T], tok)
        return tok

    def ffn(s, T, wg, wu, wd, gi_pre, gi_post, xsrc, dsts):
        nc = s.nc
        acc = s.acc3(T)
        hT = s.hT3(T)
        s.prenorm(T, gi_pre)
        wgv = wg.rearrange("(kc p) f -> p kc f", p=128)
        wuv = wu.rearrange("(kc p) f -> p kc f", p=128)
        wdv = wd.rearrange("(fc p) d -> p fc d", p=128)
        aT = s.aT[:, 0:2 * GF * T].rearrange("p (b j t) -> p b j t", b=2, j=GF)
        NFF = s.NF
        ng = (NFF + GF - 1) // GF
        s.wait('dve', s.acc_free)
        for g in range(ng):
            g0 = g * GF
            G = min(GF, NFF - g0)
            buf = g % 2
            for j in range(G):
                f = g0 + j
                bg = f % 2
                bu = 2 + f % 2
                w = s.rget(wgv[:, :, f * 128:(f + 1) * 128], 32, 128)
                s.wait('pe', s.bankfree[bg])
                for kc in range(32):
                    ins = s.mm(s.ps[bg][:, :T], w[:, kc, :], hT[:, kc, :], kc == 0, kc == 31)
                tg = s.rrel(ins)
                w = s.rget(wuv[:, :, f * 128:(f + 1) * 128], 32, 128)
                s.wait('pe', s.bankfree[bu])
                for kc in range(32):
                    ins = s.mm(s.ps[bu][:, :T], w[:, kc, :], hT[:, kc, :], kc == 0, kc == 31)
                tu = s.rrel(ins)
                s.wait('act', tg)
                s.wait('act', s.sg_free[f % 2])
                ta = s.mark('act', nc.scalar.activation(out=s.sg[:, f % 2, :T], in_=s.ps[bg][:, :T], func=AF.Silu))
                s.bankfree[bg] = ta
                s.wait('dve', tu)
                s.wait('dve', ta)
                s.wait('dve', s.aT_free[buf])
                tm = s.mark('dve', nc.vector.tensor_tensor(out=aT[:, buf, j, :], in0=s.sg[:, f % 2, :T],
                                                           in1=s.ps[bu][:, :T], op=ALU.mult))
                s.bankfree[bu] = tm
                s.sg_free[f % 2] = tm
            for dg in range(8):
                w = s.rget(wdv[:, g0:g0 + G, dg * 512:(dg + 1) * 512], G, 512)
                s.wait('pe', tm)
                for dc in range(4):
                    d = dg * 4 + dc
                    bd = 4 + d % 2
                    s.wait('pe', s.bankfree[bd])
                    for j in range(G):
                        ins = s.mm(s.ps[bd][:, :T], w[:, j, dc * 128:(dc + 1) * 128], aT[:, buf, j, :], j == 0, j == G - 1)
                    td = s.rrel(ins) if dc == 3 else s.mark('pe', ins)
                    s.wait('dve', td)
                    if g == 0:
                        te = s.mark('dve', nc.vector.tensor_copy(out=acc[:, d, :], in_=s.ps[bd][:, :T]))
                    else:
                        te = s.mark('dve', nc.vector.tensor_tensor(out=acc[:, d, :], in0=acc[:, d, :],
                                                                   in1=s.ps[bd][:, :T], op=ALU.add))
                    s.bankfree[bd] = te
            s.aT_free[buf] = td
        s.hT_free = tu
        s.acc_ready = te
        s.postnorm(T, gi_post, True, xsrc, dsts)


    def setup_mem(s, memT, wmem, gi, memkT_out, memvT_out):
        nc = s.nc
        T = 256
        s.load_x(T, memT)
        s.prenorm(T, gi)
        hT = s.hT3(T)
        wv = wmem.rearrange("(kc p) f -> p kc f", p=128)
        vT = s.aT[:, 0:8 * 256].rearrange("p (c t) -> p c t", t=256)

        def evac(oc, ps, tok):
            s.wait('act', tok)
            b = oc % 2
            s.wait('act', s.tmp_free[b])
            if oc < 8:
                nc.scalar.copy(out=s.mkT[:, oc, :], in_=ps)
            else:
                nc.scalar.copy(out=vT[:, oc - 8, :], in_=ps)
            t = s.mark('act', nc.scalar.copy(out=s.tmp32[:, b, :T], in_=ps))
            s.wait('sp', t)
            dst = (memkT_out if oc < 8 else memvT_out)[(oc % 8) * 128:(oc % 8 + 1) * 128, :]
            s.tmp_free[b] = s.dma(s.sq[b], 'sp', dst, s.tmp32[:, b, :T])
            return t
        tk = s.proj(lambda oc: wv[:, :, oc * 128:(oc + 1) * 128], 32, lambda kc: hT[:, kc, :], T, 16, evac)
        s.hT_free = tk
        last = s.bankfree[0] if s.bankfree[0][1] > s.bankfree[1][1] else s.bankfree[1]
        s.wait('pe', last)
        for kt in range(2):
            s.wait('pe', s.bankfree[7])
            for ch in range(8):
                ins = nc.tensor.transpose(s.psb[:, ch * 128:(ch + 1) * 128], vT[:, ch, kt * 128:(kt + 1) * 128], s.ident[:, :])
            t = s.mark('pe', ins)
            s.wait('dve', t)
            t = s.mark('dve', nc.vector.tensor_copy(out=s.mV[:, kt, :], in_=s.psb[:, 0:1024]))
            s.bankfree[7] = t
        for e in ('pe', 'act', 'dve', 'pool'):
            s.wait(e, t)
        s.aT_free = [t, t]
        s.acc_free = t

    def mem_attn(s, T, qm, cols, kT, V, outT, kmx):
        nc = s.nc
        c0, n = cols
        scale = 1.0 / 16.0
        for h in range(4):
            s.wait('act', s.pT_free[0])
            for i in range(2):
                ins = nc.scalar.activation(out=s.pT[:, i, :n], in_=qm[:, 2 * h + i, c0:c0 + n], func=AF.Square)
            t = s.mark('act', ins)
            s.wait('pe', t)
            s.wait('pe', s.bankfree[6])
            for i in range(2):
                ins = s.mm(s.ps[6][0:1, :n], s.ones[:, 0:1], s.pT[:, i, :n], i == 0, i == 1)
            t = s.mark('pe', ins)
            s.pT_free[0] = t
            s.wait('act', t)
            s.wait('act', s.mrow_free)
            t = s.mark('act', nc.scalar.activation(out=s.mrow[0:1, :n], in_=s.ps[6][0:1, :n], func=AF.Sqrt, scale=kmx[0:1, h:h + 1]))
            s.bankfree[6] = t
            s.wait('pe', t)
            tl = None
            for kt in range(2):
                b = kt
                s.wait('pe', s.bankfree[b])
                for i in range(2):
                    s.mm(s.ps[b][:, :n], kT[:, 2 * h + i, kt * 128:(kt + 1) * 128], qm[:, 2 * h + i, c0:c0 + n], i == 0, False)
                ins = s.mm(s.ps[b][:, :n], s.negones[0:1, :], s.mrow[0:1, :n], False, True)
                t = s.mark('pe', ins)
                s.wait('act', t)
                s.wait('act', s.pT_free[kt])
                te = s.mark('act', nc.scalar.activation(out=s.pT[:, kt, :n], in_=s.ps[b][:, :n], func=AF.Exp, scale=scale))
                s.bankfree[b] = te
            s.mrow_free = t
            s.wait('pe', te)
            for i in range(2):
                s.wait('pe', s.bankfree[2 + i])
                for kt in range(2):
                    ins = s.mm(s.ps[2 + i][:, :n], V[:, kt, h * 256 + i * 128: h * 256 + (i + 1) * 128], s.pT[:, kt, :n], kt == 0, kt == 1)
            s.wait('pe', s.bankfree[4])
            for kt in range(2):
                ins = s.mm(s.ps[4][:, :n], s.ones[:, :], s.pT[:, kt, :n], kt == 0, kt == 1)
            t = s.mark('pe', ins)
            s.pT_free = [t, t, t]
            s.wait('dve', t)
            s.wait('dve', s.rstd_free)
            t1 = s.mark('dve', nc.vector.reciprocal(out=s.rl[:, :n], in_=s.ps[4][:, :n]))
            s.bankfree[4] = t1
            s.wait('dve', t1)
            for i in range(2):
                t2 = s.mark('dve', nc.vector.tensor_tensor(out=outT(h, i), in0=s.ps[2 + i][:, :n], in1=s.rl[:, :n], op=ALU.mult))
                s.bankfree[2 + i] = t2
            s.rstd_free = t2
        return t2

    def kmax(s, kT, kmx):
        nc = s.nc
        for h in range(4):
            s.wait('act', s.pT_free[0])
            for i in range(2):
                ins = nc.scalar.activation(out=s.pT[:, i, :256], in_=kT[:, 2 * h + i, :], func=AF.Square)
            t = s.mark('act', ins)
            s.wait('pe', t)
            s.wait('pe', s.bankfree[6])
            for i in range(2):
                ins = s.mm(s.ps[6][:, :256], s.ones[:, :], s.pT[:, i, :256], i == 0, i == 1)
            t = s.mark('pe', ins)
            s.pT_free[0] = t
            s.wait('dve', t)
            t = s.mark('dve', nc.vector.tensor_reduce(out=kmx[:, h:h + 1], in_=s.ps[6][:, :256], axis=AX.X, op=ALU.max))
            s.bankfree[6] = t
        for e in ('act', 'pe', 'dve'):
            s.wait(e, t)

    def mixer_a(s, T, seqs, memsets, w_in, pool_w, w_out, xsrc, dsts, first_main, poolT_hist, poolT_out_s, poolT_out_p, last_main):
        nc = s.nc
        s.prenorm(T, 2)
        hT = s.hT3(T)
        W = sum(16 + n for (_, n, _) in seqs)
        uext = s.acc[:, 0:24 * W].rearrange("p (c w) -> p c w", w=W)
        bases = []
        b0 = 0
        for (_, n, _) in seqs:
            bases.append(b0)
            b0 += 16 + n
        wv = w_in.rearrange("(kc p) f -> p kc f", p=128)
        s.wait('dve', s.hT_ready)
        s.wait('dve', s.acc_free)
        th = None
        for si, (c0, n, hist) in enumerate(seqs):
            b = bases[si]
            if hist == 'carry':
                th = s.mark('dve', nc.vector.tensor_copy(out=uext[:, :, b + 1:b + 16], in_=s.carry[:, :, 1:16]))
            elif hist is None:
                th = s.mark('dve', nc.vector.memset(uext[:, :, b:b + 16], 0.0))
            else:
                s.wait('sp', s.hT_ready)
                s.wait('sp', s.acc_free)
                with nc.allow_non_contiguous_dma(reason="pool history"):
                    th = s.dma(s.cq, 'sp', uext[:, :, b + 1:b + 16], poolT_hist[:, hist, :].rearrange("(c p) j -> p c j", p=128))
                s.wait('dve', th)

        def evac_in(oc, ps, tok):
            s.wait('act', tok)
            if oc < 24:
                for si, (c0, n, hist) in enumerate(seqs):
                    b = bases[si]
                    ins = nc.scalar.copy(out=uext[:, oc, b + 16:b + 16 + n], in_=ps[:, c0:c0 + n])
            else:
                ins = nc.scalar.copy(out=s.qm[:, oc - 24, :T], in_=ps)
            return s.mark('act', ins)
        s.wait('act', s.hT_ready)
        s.wait('act', s.acc_free)
        s.wait('act', s.qm_free)
        tk = s.proj(lambda oc: wv[:, :, oc * 128:(oc + 1) * 128], 32, lambda kc: hT[:, kc, :], T, 32, evac_in)
        s.hT_free = tk
        tin = s.bankfree[1] if s.bankfree[1][1] > s.bankfree[0][1] else s.bankfree[0]
        mixT = s.hT3(T)
        s.wait('dve', tk)
        s.wait('act', tk)
        tm = None
        for (cols, kT, V, kmx) in memsets:
            c0, n = cols
            tm = s.mem_attn(T, s.qm, cols, kT, V, lambda h, i, c0=c0, n=n: mixT[:, 24 + 2 * h + i, c0:c0 + n], kmx)
        s.qm_free = tm
        covered = sorted(cols for (cols, _, _, _) in memsets)
        if covered[0][0] > 0:
            nc.vector.memset(mixT[:, 24:32, 0:covered[0][0]], 0.0)
        s.wait('dve', tin)
        s.wait('dve', th)
        dT = s.aT[:, 0:2 * 6 * T].rearrange("p (b j t) -> p b j t", b=2, j=6)
        for gi in range(4):
            w = 2 ** (gi + 1)
            buf = gi % 2
            s.wait('dve', s.aT_free[buf])
            for j in range(6):
                oc = gi * 6 + j
                cur = uext[:, oc, :]
                k = 0
                step = 1
                while step < w:
                    nxt = s.ptmp[:, k % 2, :W]
                    t = s.mark('dve', nc.vector.tensor_tensor(out=nxt[:, step:W], in0=cur[:, step:W], in1=cur[:, 0:W - step], op=ALU.add))
                    s.wait('dve', t)
                    cur = nxt
                    step *= 2
                    k += 1
                for si, (c0, n, hist) in enumerate(seqs):
                    b = bases[si]
                    t = s.mark('dve', nc.vector.scalar_tensor_tensor(out=dT[:, buf, j, c0:c0 + n], in0=cur[:, b + 16:b + 16 + n], scalar=1.0 / w,
                                                                     in1=uext[:, oc, b + 16:b + 16 + n], op0=ALU.mult, op1=ALU.subtract))
                if first_main:
                    b = bases[0]
                    s.wait('dve', t)
                    t = s.mark('dve', nc.vector.tensor_tensor(out=s.ptmp[:, 0, 0:16], in0=cur[:, b + 16:b + 32], in1=s.invc[:, gi * 16:(gi + 1) * 16], op=ALU.mult))
                    s.wait('dve', t)
                    t = s.mark('dve', nc.vector.tensor_tensor(out=dT[:, buf, j, 0:16], in0=s.ptmp[:, 0, 0:16], in1=uext[:, oc, b + 16:b + 32], op=ALU.subtract))
                    s.wait('dve', t)
            s.wait('pe', t)
            pwv = pool_w[gi].rearrange("(cc p) d -> p cc d", p=128)

            def evac_pw(dcl, ps, tok, gi=gi):
                s.wait('act', tok)
                ch = gi * 6 + dcl
                return s.mark('act', nc.scalar.activation(out=mixT[:, ch, :], in_=ps, func=AF.Copy, scale=s.psc[:, ch:ch + 1]))
            tk2 = s.proj(lambda dcl: pwv[:, :, dcl * 128:(dcl + 1) * 128], 6, lambda cc, buf=buf: dT[:, buf, cc, :], T, 6, evac_pw)
            s.aT_free[buf] = tk2
        tcar = None
        for si, (c0, n, hist) in enumerate(seqs):
            b = bases[si]
            if hist == 'carry':
                tcar = s.mark('dve', nc.vector.tensor_copy(out=s.carry[:, :, 1:16], in_=uext[:, :, b + 16 + n - 15:b + 16 + n]))
                if last_main:
                    s.wait('sp', tcar)
                    with nc.allow_non_contiguous_dma(reason="pool state out"):
                        s.dma(s.oq, 'sp', poolT_out_p.rearrange("(c p) j -> p c j", p=128), s.carry[:, :, 1:16])
            elif hist is None:
                tcar = s.mark('dve', nc.vector.tensor_scalar(out=s.carry[:, :, 1:16], in0=uext[:, :, b + 16:b + 31], scalar1=s.halo_on[:, 0:1], scalar2=None, op0=ALU.mult))
            else:
                s.wait('sp', tin)
                with nc.allow_non_contiguous_dma(reason="pool state out"):
                    tcar = s.dma(s.oq, 'sp', poolT_out_s[:, hist, :].rearrange("(c p) j -> p c j", p=128), uext[:, :, b + 17:b + 32])
        act_last = s.bankfree[0] if s.bankfree[0][1] > s.bankfree[1][1] else s.bankfree[1]
        s.wait('pe', act_last)
        s.wait('pe', tm)
        wov = w_out.rearrange("(kc p) f -> p kc f", p=128)
        acc = s.acc3(T)

        def evac_out(oc, ps, tok):
            s.wait('dve', tok)
            s.wait('dve', tcar)
            return s.mark('dve', nc.vector.tensor_copy(out=acc[:, oc, :], in_=ps))
        tk3 = s.proj(lambda oc: wov[:, :, oc * 128:(oc + 1) * 128], 32, lambda kc: mixT[:, kc, :], T, 32, evac_out)
        s.hT_free = tk3
        s.acc_ready = s.bankfree[1] if s.bankfree[1][1] > s.bankfree[0][1] else s.bankfree[0]
        s.postnorm(T, 3, False, xsrc, dsts)

    def kvproj(s, T, c0, wkv, kT_out, vT_out, lfT_out):
        nc = s.nc
        s.prenorm(T, 6)
        hT = s.hT3(T)
        wv = wkv.rearrange("(kc p) f -> p kc f", p=128)

        def evac(oc, ps, tok):
            b = oc % 2
            s.wait('act', tok)
            s.wait('act', s.tmp_free[b])
            t = s.mark('act', nc.scalar.copy(out=s.tmp32[:, b, :T], in_=ps))
            s.wait('sp', t)
            dst = (kT_out if oc < 24 else vT_out)[(oc % 24) * 128:(oc % 24 + 1) * 128, c0:c0 + T]
            s.tmp_free[b] = s.dma(s.sq[b], 'sp', dst, s.tmp32[:, b, :T])
            return t
        s.proj(lambda oc: wv[:, :, oc * 128:(oc + 1) * 128], 32, lambda kc: hT[:, kc, :], T, 48, evac)
        s.wait('pe', s.bankfree[6])
        for kc in range(32):
            ins = s.mm(s.ps[6][0:24, :T], s.wf_sb[:, kc, :], hT[:, kc, :], kc == 0, kc == 31)
        t = s.mark('pe', ins)
        s.hT_free = t
        s.wait('act', t)
        s.wait('act', s.lf_free)
        s.wait('act', s.sg_free[0])
        s.wait('act', s.sg_free[1])
        z = s.lf[0:24, 0, :T]
        ab = s.lf[0:24, 1, :T]
        t = s.mark('act', nc.scalar.activation(out=z, in_=s.ps[6][0:24, :T], func=AF.Identity, bias=s.bf_sb[0:24, 0:1]))
        s.bankfree[6] = t
        s.wait('act', t)
        t = s.mark('act', nc.scalar.activation(out=ab, in_=z, func=AF.Abs))
        s.wait('act', t)
        t = s.mark('act', nc.scalar.activation(out=ab, in_=ab, func=AF.Exp, scale=-1.0))
        s.wait('act', t)
        t = s.mark('act', nc.scalar.activation(out=ab, in_=ab, func=AF.Ln, bias=s.onecol[0:24, 0:1]))
        s.wait('dve', t)
        t = s.mark('dve', nc.vector.scalar_tensor_tensor(out=z, in0=z, scalar=0.0, in1=ab, op0=ALU.min, op1=ALU.subtract))
        s.wait('sp', t)
        s.lf_free = s.dma(s.oq, 'sp', lfT_out[:, c0:c0 + T], z)


    def fox(s, Tq, qh, past, diag, out_ap, KB2=289.0, drow=None):
        nc = s.nc
        scale = 128.0 ** -0.5
        s.wait('act', s.pT_free[0])
        t = s.mark('act', nc.scalar.activation(out=s.pT[:, 0, :Tq], in_=qh, func=AF.Square))
        s.wait('pe', t)
        s.wait('pe', s.bankfree[6])
        t = s.mark('pe', s.mm(s.ps[6][0:1, :Tq], s.ones[:, 0:1], s.pT[:, 0, :Tq], True, True))
        s.pT_free[0] = t
        s.wait('act', t)
        s.wait('act', s.mrow_free)
        if drow is None:
            t = s.mark('act', nc.scalar.activation(out=s.mrow[0:1, :Tq], in_=s.ps[6][0:1, :Tq], func=AF.Sqrt, scale=KB2))
            s.bankfree[6] = t
        else:
            s.wait('act', s.m32_free)
            s.wait('act', s.sg_free[0])
            t = s.mark('act', nc.scalar.activation(out=s.mrow32[0:1, :Tq], in_=s.ps[6][0:1, :Tq], func=AF.Sqrt, scale=KB2))
            s.wait('pe', t)
            for r in range(4):
                tp = s.mark('pe', s.mm(s.ps[6][0:1, r * 128:(r + 1) * 128], s.dbias_bf[:, drow + r:drow + r + 1], s.ident[:, :], True, True))
            s.wait('dve', tp)
            t = s.mark('dve', nc.vector.scalar_tensor_tensor(out=s.mrow[0:1, :Tq], in0=s.ps[6][0:1, :Tq], scalar=1.0 / scale,
                                                             in1=s.mrow32[0:1, :Tq], op0=ALU.mult, op1=ALU.add))
            s.bankfree[6] = t
            s.m32_free = t
            s.sg_free[0] = t
        s.wait('pe', t)
        items = [('d', d) for d in diag] + [('p', kt) for kt in range(past[3])]
        n = len(items)
        s.wait('pe', s.bankfree[2])
        s.wait('pe', s.bankfree[4])
        for idx, (kind, it) in enumerate(items):
            b = idx % 2
            pb = idx % 3
            if kind == 'd':
                kTd, Vd, biasd, m, q0, mask = it
            else:
                kt = it
                kTd = past[0][:, kt * 128:(kt + 1) * 128]
                Vd = past[1][:, kt, :]
                biasd = past[2][:, kt:kt + 1]
                m, q0, mask = 128, 0, None
            s.wait('pe', s.bankfree[b])
            s.mm(s.ps[b][0:m, q0:Tq], kTd, qh[:, q0:Tq], True, False)
            t = s.mark('pe', s.mm(s.ps[b][0:m, q0:Tq], s.negones[0:1, 0:m], s.mrow[0:1, q0:Tq], False, True))
            s.wait('act', t)
            s.wait('act', s.pT_free[pb])
            te = s.mark('act', nc.scalar.activation(out=s.pT[0:m, pb, q0:Tq], in_=s.ps[b][0:m, q0:Tq], func=AF.Exp, scale=scale, bias=biasd))
            s.bankfree[b] = te
            if mask is not None:
                s.wait('dve', te)
                te = s.mark('dve', nc.vector.tensor_tensor(out=s.pT[0:m, pb, q0:q0 + m], in0=s.pT[0:m, pb, q0:q0 + m], in1=mask, op=ALU.mult))
            s.wait('pe', te)
            s.mm(s.ps[2][:, q0:Tq], Vd, s.pT[0:m, pb, q0:Tq], idx == 0, idx == n - 1)
            t = s.mark('pe', s.mm(s.ps[4][:, q0:Tq], s.ones[0:m, :], s.pT[0:m, pb, q0:Tq], idx == 0, idx == n - 1))
            s.pT_free[pb] = t
        s.mrow_free = t
        s.wait('dve', t)
        s.wait('dve', s.rstd_free)
        t1 = s.mark('dve', nc.vector.reciprocal(out=s.rl[:, :Tq], in_=s.ps[4][:, :Tq]))
        s.bankfree[4] = t1
        s.wait('dve', t1)
        t2 = s.mark('dve', nc.vector.tensor_tensor(out=out_ap, in0=s.ps[2][:, :Tq], in1=s.rl[:, :Tq], op=ALU.mult))
        s.bankfree[2] = t2
        s.rstd_free = t2
        return t, t2

    def fox_jobs(s, jobs, mixT, qs, T, j, KB2=289.0):
        nc = s.nc
        scale = 128.0 ** -0.5
        accb = s.acc[:].bitcast(BF16)
        CK = 32
        ringK = [accb[:, b * 8192:b * 8192 + 4096] for b in range(4)]
        ringV = [accb[:, b * 8192 + 4096:(b + 1) * 8192].rearrange("p (k f) -> p k f", f=128) for b in range(4)]
        chunks = []
        for ji, jb in enumerate(jobs):
            k0 = 0
            while k0 < jb["nk"]:
                n = min(CK, jb["nk"] - k0)
                chunks.append((ji, k0, n))
                k0 += n
        st = dict(next_chunk=0, next_small=0)
        chunk_tok = {}
        small_tok = {}

        def issue_small(ji):
            jb = jobs[ji]
            sb = ji % 2
            s.wait('pool', s.small_free[sb])
            n = jb["n"]
            s.dma(s.kvq2[sb], 'pool', s.qh[:, sb, :n], qs[jb["h"] * 128:(jb["h"] + 1) * 128, jb["c0"]:jb["c0"] + n])
            s.dma(s.kvq2[sb], 'pool', s.bias_sb[:, sb, 0:jb["nk"]], jb["b"])
            with nc.allow_non_contiguous_dma(reason="small per-head slices"):
                if jb["sq"] is None:
                    s.dma(s.kvq2[sb], 'pool', s.kd[:, sb, 0:512], jb["kd"])
                    t = s.dma(s.kvq2[sb], 'pool', s.vd[:, sb, :, :], jb["vd"])
                else:
                    s.dma(s.kvq2[sb], 'pool', s.kd[:, sb, 0:16], jb["kd"])
                    t = s.dma(s.kvq2[sb], 'pool', s.vd[0:16, sb, 0, :], jb["vd"].rearrange("f t -> t f"))
            small_tok[ji] = t

        def issue_chunk(ci):
            ji, k0, n = chunks[ci]
            jb = jobs[ji]
            rb = ci % 4
            s.wait('pool', s.ring_free[rb])
            s.dma(s.kvq4[rb], 'pool', ringK[rb][:, 0:n * 128], jb["kT"][:, k0 * 128:(k0 + n) * 128])
            t = s.dma(s.kvq4[rb], 'pool', ringV[rb][:, 0:n, :], jb["v"][:, k0:k0 + n, :])
            chunk_tok[ci] = t

        def ensure(ci_upto, ji_upto):
            while st["next_small"] <= min(ji_upto, len(jobs) - 1):
                issue_small(st["next_small"])
                st["next_small"] += 1
            while st["next_chunk"] <= min(ci_upto, len(chunks) - 1):
                issue_chunk(st["next_chunk"])
                st["next_chunk"] += 1

        ci = 0
        tpe = tdv = None
        for ji, jb in enumerate(jobs):
            ensure(ci + 2, ji + 1)
            sb = ji % 2
            n = jb["n"]
            h = jb["h"]
            qh = s.qh[:, sb, :n]
            for e in ('act', 'pe', 'dve'):
                s.wait(e, small_tok[ji])
            s.wait('act', s.pT_free[0])
            t = s.mark('act', nc.scalar.activation(out=s.pT[:, 0, :n], in_=qh, func=AF.Square))
            s.wait('pe', t)
            s.wait('pe', s.bankfree[6])
            t = s.mark('pe', s.mm(s.ps[6][0:1, :n], s.ones[:, 0:1], s.pT[:, 0, :n], True, True))
            s.pT_free[0] = t
            s.wait('act', t)
            s.wait('act', s.mrow_free)
            if jb["sq"] is not None:
                t = s.mark('act', nc.scalar.activation(out=s.mrow[0:1, :n], in_=s.ps[6][0:1, :n], func=AF.Sqrt, scale=KB2))
                s.bankfree[6] = t
            else:
                s.wait('act', s.m32_free)
                s.wait('act', s.sg_free[0])
                t = s.mark('act', nc.scalar.activation(out=s.mrow32[0:1, :n], in_=s.ps[6][0:1, :n], func=AF.Sqrt, scale=KB2))
                s.wait('pe', t)
                drow = h * 16 + 4 * j
                for r in range(4):
                    tp = s.mark('pe', s.mm(s.ps[6][0:1, r * 128:(r + 1) * 128], s.dbias_bf[:, drow + r:drow + r + 1], s.ident[:, :], True, True))
                s.wait('dve', tp)
                t = s.mark('dve', nc.vector.scalar_tensor_tensor(out=s.mrow[0:1, :n], in0=s.ps[6][0:1, :n], scalar=1.0 / scale,
                                                                 in1=s.mrow32[0:1, :n], op0=ALU.mult, op1=ALU.add))
                s.bankfree[6] = t
                s.m32_free = t
                s.sg_free[0] = t
            s.wait('pe', t)
            items = []
            if jb["sq"] is None:
                for r in range(4):
                    items.append(dict(kT=s.kd[:, sb, r * 128:(r + 1) * 128], V=s.vd[:, sb, r, :],
                                      bias=s.dbias[:, h * 16 + 4 * j + r:h * 16 + 4 * j + r + 1], m=128, q0=128 * r, mask=s.tri[:, :], rel=None))
            else:
                sq_ = jb["sq"]
                items.append(dict(kT=s.kd[:, sb, 0:16], V=s.vd[0:16, sb, 0, :], bias=s.dbias_s[0:16, sq_ * 24 + h:sq_ * 24 + h + 1],
                                  m=16, q0=0, mask=s.tri[0:16, 0:16], rel=None))
            my_chunks = []
            while ci < len(chunks) and chunks[ci][0] == ji:
                my_chunks.append(ci)
                ci += 1
            for cc in my_chunks:
                _, k0, nn = chunks[cc]
                rb = cc % 4
                for kk in range(nn):
                    items.append(dict(kT=ringK[rb][:, kk * 128:(kk + 1) * 128], V=ringV[rb][:, kk, :],
                                      bias=s.bias_sb[:, sb, k0 + kk:k0 + kk + 1], m=128, q0=0, mask=None,
                                      rel=(cc if kk == nn - 1 else None), need=(cc if kk == 0 else None)))
            nI = len(items)
            s.wait('pe', s.bankfree[2])
            s.wait('pe', s.bankfree[4])
            stoks = {}

            def emit_S(i):
                it = items[i]
                b = i % 2
                m, q0 = it["m"], it["q0"]
                if it.get("need") is not None:
                    ensure(it["need"] + 2, ji + 1)
                    s.wait('pe', chunk_tok[it["need"]])
                s.wait('pe', s.bankfree[b])
                s.mm(s.ps[b][0:m, q0:n], it["kT"], qh[:, q0:n], True, False)
                stoks[i] = s.mark('pe', s.mm(s.ps[b][0:m, q0:n], s.negones[0:1, 0:m], s.mrow[0:1, q0:n], False, True))

            emit_S(0)
            for i in range(nI):
                it = items[i]
                b = i % 2
                pb = i % 3
                m, q0 = it["m"], it["q0"]
                if i + 1 < nI:
                    emit_S(i + 1)
                s.wait('act', stoks[i])
                s.wait('act', s.pT_free[pb])
                te = s.mark('act', nc.scalar.activation(out=s.pT[0:m, pb, q0:n], in_=s.ps[b][0:m, q0:n], func=AF.Exp, scale=scale, bias=it["bias"]))
                s.bankfree[b] = te
                if it["mask"] is not None:
                    s.wait('dve', te)
                    te = s.mark('dve', nc.vector.tensor_tensor(out=s.pT[0:m, pb, q0:q0 + m], in0=s.pT[0:m, pb, q0:q0 + m], in1=it["mask"], op=ALU.mult))
                s.wait('pe', te)
                s.mm(s.ps[2][:, q0:n], it["V"], s.pT[0:m, pb, q0:n], i == 0, i == nI - 1)
                tpe = s.mark('pe', s.mm(s.ps[4][:, q0:n], s.ones[0:m, :], s.pT[0:m, pb, q0:n], i == 0, i == nI - 1))
                s.pT_free[pb] = tpe
                if it.get("rel") is not None:
                    s.ring_free[it["rel"] % 4] = tpe
            s.mrow_free = tpe
            s.small_free[sb] = tpe
            s.wait('dve', tpe)
            s.wait('dve', s.rstd_free)
            t1 = s.mark('dve', nc.vector.reciprocal(out=s.rl[:, :n], in_=s.ps[4][:, :n]))
            s.bankfree[4] = t1
            s.wait('dve', t1)
            tdv = s.mark('dve', nc.vector.tensor_tensor(out=mixT[:, h, jb["c0"]:jb["c0"] + n], in0=s.ps[2][:, :n], in1=s.rl[:, :n], op=ALU.mult))
            s.bankfree[2] = tdv
            s.rstd_free = tdv
        return tpe, tdv

    def bias_tables(s, lf_src, vis_src, nk, dst, wtri_dst=None):
        nc = s.nc
        N = 24 * nk
        A = lambda k: s.acc[:, k * 3072:k * 3072 + N]
        lfb = s.hT[:, 0:N]
        s.wait('sp', s.acc_free)
        t = s.dma(s.xq[0], 'sp', A(0), lf_src)
        if vis_src is not None:
            t = s.dma(s.xq[0], 'sp', A(3), vis_src)
        s.wait('dve', t)
        s.wait('dve', s.hT_free)
        t = s.mark('dve', nc.vector.tensor_copy(out=lfb, in_=A(0)))
        s.wait('dve', t)
        lfl = s.hT[:, 4096:4096 + N]
        t = s.mark('dve', nc.vector.tensor_tensor(out=lfl, in0=A(0), in1=lfb, op=ALU.subtract))
        s.wait('pe', t)
        tw = None
        for cch in range((N + 511) // 512):
            c0 = cch * 512
            c1 = min(N, c0 + 512)
            for (bk, lhs, dk) in ((0, s.triu, 1), (1, s.ones, 2)):
                s.wait('pe', s.bankfree[bk])
                s.mm(s.ps[bk][:, 0:c1 - c0], lhs[:, :], lfb[:, c0:c1], True, False)
                tm = s.mark('pe', s.mm(s.ps[bk][:, 0:c1 - c0], lhs[:, :], lfl[:, c0:c1], False, True))
                s.wait('dve', tm)
                tw = s.mark('dve', nc.vector.tensor_copy(out=A(dk)[:, c0:c1], in_=s.ps[bk][:, 0:c1 - c0]))
                s.bankfree[bk] = tw
        s.hT_free = tm
        s.wait('dve', tw)
        cur = A(2)
        if vis_src is not None:
            t = s.mark('dve', nc.vector.tensor_tensor(out=A(4), in0=A(2), in1=A(3), op=ALU.mult))
            s.wait('dve', t)
            cur = A(4)
        step = 1
        k = 0
        bufs = [A(2) if vis_src is not None else A(4), A(0)]
        while step < nk:
            nxt = bufs[k % 2]
            c3 = cur.rearrange("p (h k) -> p h k", k=nk)
            n3 = nxt.rearrange("p (h k) -> p h k", k=nk)
            nc.vector.tensor_tensor(out=n3[:, :, 0:nk - step], in0=c3[:, :, 0:nk - step], in1=c3[:, :, step:nk], op=ALU.add)
            t = s.mark('dve', nc.vector.tensor_copy(out=n3[:, :, nk - step:nk], in_=c3[:, :, nk - step:nk]))
            s.wait('dve', t)
            cur = nxt
            step *= 2
            k += 1
        res = A(1)
        t = s.mark('dve', nc.vector.tensor_tensor(out=res, in0=cur, in1=A(1), op=ALU.subtract))
        s.wait('dve', t)
        if vis_src is not None:
            t = s.mark('dve', nc.vector.tensor_scalar(out=A(3), in0=A(3), scalar1=BIG, scalar2=-BIG, op0=ALU.mult, op1=ALU.add))
            s.wait('dve', t)
            t = s.mark('dve', nc.vector.tensor_tensor(out=res, in0=res, in1=A(3), op=ALU.add))
        s.wait('sp', t)
        s.acc_free = s.dma(s.stq, 'sp', dst, res)
        s.acc_ready = s.acc_free

    def diag_bias(s, lf_src, rows, nk, group, dst_sb):
        nc = s.nc
        N = 24 * nk
        A = lambda k: s.acc[0:rows, k * 3072:k * 3072 + N]
        lfb = s.hT[0:rows, 0:N]
        s.wait('sp', s.acc_free)
        t = s.dma(s.xq[0], 'sp', A(0), lf_src)
        s.wait('dve', t)
        s.wait('dve', s.hT_free)
        t = s.mark('dve', nc.vector.tensor_copy(out=lfb, in_=A(0)))
        s.wait('dve', t)
        lfl = s.hT[0:rows, 4096:4096 + N]
        t = s.mark('dve', nc.vector.tensor_tensor(out=lfl, in0=A(0), in1=lfb, op=ALU.subtract))
        s.wait('pe', t)
        for (bk, lhs, dk) in ((0, s.triu, 1), (1, s.ones, 2)):
            s.wait('pe', s.bankfree[bk])
            s.mm(s.ps[bk][0:rows, 0:N], lhs[0:rows, 0:rows], lfb, True, False)
            tm = s.mark('pe', s.mm(s.ps[bk][0:rows, 0:N], lhs[0:rows, 0:rows], lfl, False, True))
            s.wait('dve', tm)
            tw = s.mark('dve', nc.vector.tensor_copy(out=A(dk), in_=s.ps[bk][0:rows, 0:N]))
            s.bankfree[bk] = tw
        s.hT_free = tm
        s.wait('dve', tw)
        W3 = A(1).rearrange("p (h k) -> p h k", k=nk)
        T3 = A(2).rearrange("p (h k) -> p h k", k=nk)
        d3 = dst_sb
        t = None
        for kt in range(nk):
            g0 = (kt // group) * group
            t = s.mark('dve', nc.vector.tensor_scalar(out=d3[0:rows, :, kt:kt + 1], in0=W3[:, :, kt:kt + 1], scalar1=-1.0, scalar2=None, op0=ALU.mult))
            for k2 in range(g0, kt):
                s.wait('dve', t)
                t = s.mark('dve', nc.vector.tensor_tensor(out=d3[0:rows, :, kt:kt + 1], in0=d3[0:rows, :, kt:kt + 1], in1=T3[:, :, k2:k2 + 1], op=ALU.subtract))
        s.wait('dve', t)
        for e in ('act', 'pe', 'sp'):
            s.wait(e, t)
        s.acc_free = t
        s.acc_ready = t

    def mixer_b(s, T, is_e, j, w_in, w_out, xsrc, dsts, B):
        nc = s.nc
        s.prenorm(T, 2)
        hT = s.hT3(T)
        wv = w_in.rearrange("(kc p) f -> p kc f", p=128)
        qs = B["qs"]

        def evac_in(oc, ps, tok):
            s.wait('act', tok)
            if oc < 24:
                b = oc % 2
                s.wait('act', s.tmp_free[b])
                t = s.mark('act', nc.scalar.copy(out=s.tmp32[:, b, :T], in_=ps))
                s.wait('sp', t)
                s.tmp_free[b] = s.dma(s.sq[b], 'sp', qs[oc * 128:(oc + 1) * 128, 0:T], s.tmp32[:, b, :T])
                return t
            return s.mark('act', nc.scalar.copy(out=s.qm[:, oc - 24, :T], in_=ps))
        s.wait('act', s.qm_free)
        tk = s.proj(lambda oc: wv[:, :, oc * 128:(oc + 1) * 128], 32, lambda kc: hT[:, kc, :], T, 32, evac_in)
        s.hT_free = tk
        mixT = s.hT3(T)
        s.wait('dve', tk)
        s.wait('act', tk)
        tlast = s.bankfree[1] if s.bankfree[1][1] > s.bankfree[0][1] else s.bankfree[0]
        if is_e:
            memsets = [((0, 16), s.smk[0], s.smv[0], s.kmx[:, 4:8]), ((16, 16), s.smk[1], s.smv[1], s.kmx[:, 8:12])]
        else:
            memsets = [((0, T), s.mkT, s.mV, s.kmx[:, 0:4])]
        s.wait('act', tlast)
        s.wait('pe', tlast)
        tm = None
        for (cols, kT, V, kmx) in memsets:
            c0, n = cols
            tm = s.mem_attn(T, s.qm, cols, kT, V, lambda h, i, c0=c0, n=n: mixT[:, 24 + 2 * h + i, c0:c0 + n], kmx)
        s.qm_free = tm
        jobs = []
        for h in range(NH):
            if is_e:
                for sq_ in range(2):
                    jobs.append(dict(h=h, c0=16 * sq_, n=16, nk=16, sq=sq_,
                                     kT=B["kTc"][sq_][h * 128:(h + 1) * 128, :], v=B["vc"][sq_][h],
                                     b=B["biasC"][sq_][:, h * 16:(h + 1) * 16],
                                     kd=B["kTn"][sq_][h * 128:(h + 1) * 128, :], vd=B["vn"][sq_][h * 128:(h + 1) * 128, :]))
            else:
                nk = NPAST[j]
                jobs.append(dict(h=h, c0=0, n=T, nk=nk, sq=None,
                                 kT=B["kT_all"][h * 128:(h + 1) * 128, 0:nk * 128], v=B["v_all"][h][:, 0:nk, :],
                                 b=B["biasD"][j][:, h * 128:h * 128 + nk],
                                 kd=B["kT_own"][h * 128:(h + 1) * 128, j * 512:(j + 1) * 512], vd=B["v_own"][h][:, 4 * j:4 * j + 4, :]))
        for q in (s.sq[0], s.sq[1]):
            s.wait('pool', (q, q[1]))
        s.wait('pool', tm)
        s.wait('pool', s.acc_free)
        tpe, tdv = s.fox_jobs(jobs, mixT, B["qs"], T, j)
        s.acc_free = tpe
        wov = w_out.rearrange("(kc p) f -> p kc f", p=128)
        acc = s.acc3(T)
        s.wait('pe', tdv)
        s.wait('pe', tm)

        def evac_out(oc, ps, tok):
            s.wait('dve', tok)
            return s.mark('dve', nc.vector.tensor_copy(out=acc[:, oc, :], in_=ps))
        tk3 = s.proj(lambda oc: wov[:, :, oc * 128:(oc + 1) * 128], 32, lambda kc: mixT[:, kc, :], T, 32, evac_out)
        s.hT_free = tk3
        s.acc_ready = s.bankfree[1] if s.bankfree[1][1] > s.bankfree[0][1] else s.bankfree[0]
        s.acc_free = None
        s.postnorm(T, 3, False, xsrc, dsts)


def build_A(nf=NF, tiles=None, parts=("ffn1", "mix", "ffn2", "kv")):
    global R
    R = 5
    c = Ctx(nf)
    nc = c.nc
    TT = TE_A + TP
    dt = lambda name, shape, kind="ExternalInput": nc.dram_tensor(name, shape, F32, kind=kind).ap()
    xT = dt("xT", [D, TT])
    gains = dt("gains", [128, 7 * 32])
    gmem = dt("gmem", [128, 7 * 32])
    wg1 = dt("wg1", [D, DFF]); wu1 = dt("wu1", [D, DFF]); wd1 = dt("wd1", [DFF, D])
    wg2 = dt("wg2", [D, DFF]); wu2 = dt("wu2", [D, DFF]); wd2 = dt("wd2", [DFF, D])
    w_in = dt("w_in", [D, D]); pool_w = dt("pool_w", [4, 768, 768]); w_out = dt("w_out", [D, D])
    psc_in = dt("psc", [128, 24])
    wkv = dt("wkv", [D, 2 * DPOOL]); wf = dt("wf", [D, NH]); bfv = dt("bf", [NH, 1])
    memT = dt("memT", [D, 256]); wmem0 = dt("wmem0", [D, 2048]); wmem1 = dt("wmem1", [D, 2048])
    cmkT = dt("cmkT", [2, 1024, 256]); cmv = dt("cmv", [2, 256, 1024])
    poolT_hist = dt("poolT_hist", [DPOOL, 2, 15])
    meta = dt("meta", [128, 80])
    xo = dt("xo", [D, TT], "ExternalOutput")
    kT_out = dt("kT_out", [DPOOL, TT], "ExternalOutput"); vT_out = dt("vT_out", [DPOOL, TT], "ExternalOutput")
    lfT_out = dt("lfT_out", [NH, TT], "ExternalOutput")
    memk0 = dt("memk0", [1024, 256], "ExternalOutput"); memv0 = dt("memv0", [1024, 256], "ExternalOutput")
    memk1 = dt("memk1", [1024, 256], "ExternalOutput"); memv1 = dt("memv1", [1024, 256], "ExternalOutput")
    poolT_out_s = dt("poolT_out_s", [DPOOL, 2, 15], "ExternalOutput"); poolT_out_p = dt("poolT_out_p", [DPOOL, 15], "ExternalOutput")
    xs1 = nc.dram_tensor("xs1", [D, TT], F32).ap()
    xs2 = nc.dram_tensor("xs2", [D, TT], F32).ap()
    c.mkT = c.sb("mkT", [128, 8, 256], BF16); c.mV = c.sb("mV", [128, 2, 1024], BF16)
    c.qm = c.sb("qm", [128, 8, 512], BF16)
    c.pT = c.sb("pT", [128, 3, 512], BF16)
    c.mrow = c.sb("mrow", [1, 512], BF16)
    c.rl = c.rstd
    c.ptmp = c.sb("ptmp", [128, 2, 544], F32)
    c.carry = c.sb("carry", [128, 24, 16], F32)
    c.psc = c.sb("pscs", [128, 24], F32)
    c.metas = c.sb("metas", [128, 80], F32)
    c.kmx = c.sb("kmx", [128, 12], F32)
    c.ident = c.sb("ident", [128, 128], BF16)
    c.wf_sb = c.sb("wf_sb", [128, 32, NH], BF16)
    c.bf_sb = c.sb("bf_sb", [NH, 1], F32)
    c.lf = c.sg
    c.psb = c.st.enter_context(nc.psum_tensor("psb", [128, 2048], BF16)) if False else None
    c.pT_free = [None, None, None]; c.mrow_free = None; c.rl_free = None; c.qm_free = None; c.lf_free = None
    c.invc = c.metas[:, 0:64]; c.halo_on = c.metas[:, 64:65]
    c.consts(gmem)
    t = c.dma(c.cq, 'sp', c.psc[:], psc_in)
    t = c.dma(c.cq, 'sp', c.metas[:], meta)
    t = c.dma(c.cq, 'sp', c.bf_sb[:], bfv)
    with nc.allow_non_contiguous_dma(reason="small w_f"):
        t2 = c.dma(c.cq, 'pool', c.wf_sb[:], wf.rearrange("(kc p) h -> p kc h", p=128))
    for e in ('pe', 'act', 'dve'):
        c.wait(e, t2)
    ti = c.mark('pool', nc.gpsimd.affine_select(out=c.ident[:], in_=c.ones[:], pattern=[[-1, 128]], compare_op=ALU.is_equal, fill=0.0, base=0, channel_multiplier=1))
    c.wait('pe', ti)
    smb = c.acc[:, 12288:16384].bitcast(BF16)
    c.smk = [smb[:, i * 2048:(i + 1) * 2048].rearrange("p (c s) -> p c s", s=256) for i in range(2)]
    c.smv = [smb[:, 4096 + i * 2048:4096 + (i + 1) * 2048].rearrange("p (k f) -> p k f", f=1024) for i in range(2)]
    for i in range(2):
        c.dma(c.cq, 'pool', c.smk[i], cmkT[i].rearrange("(c p) s -> p c s", p=128))
        t2 = c.dma(c.cq, 'pool', c.smv[i], cmv[i].rearrange("(k p) f -> p k f", p=128))
    for e in ('pe', 'act', 'dve'):
        c.wait(e, t2)
    c.psb = c.ps[7].bitcast(BF16) if hasattr(c.ps[7], "bitcast") else None
    c.setup_mem(memT, wmem1, 1, memk1, memv1)
    c.setup_mem(memT, wmem0, 0, memk0, memv0)
    c.kmax(c.mkT, c.kmx[:, 0:4])
    c.kmax(c.smk[0], c.kmx[:, 4:8])
    c.kmax(c.smk[1], c.kmx[:, 8:12])
    t = c.dma(c.cq, 'sp', c.gv[:], gains)
    c.wait('dve', t)
    tk = c.mark('dve', nc.vector.tensor_scalar(out=c.gvh[:], in0=c.gv[:], scalar1=0.5, scalar2=None, op0=ALU.mult))
    for e in ('pe', 'act', 'dve'):
        c.wait(e, tk)
    if tiles is None:
        tiles = [(0, TE_A)] + [(TE_A + 512 * i, 512) for i in range(4)]
    for ti_, (c0, T) in enumerate(tiles):
        is_e = (c0 == 0)
        cs = slice(c0, c0 + T)
        c.load_x(T, xT[:, cs])
        if "ffn1" in parts:
            c.ffn(T, wg1, wu1, wd1, 0, 1, xT[:, cs], [xs1[:, cs]])
        if "mix" in parts:
            if is_e:
                seqs = [(0, 15, None), (15, 16, 0), (31, 16, 1)]
                memsets = [((15, 16), c.smk[0], c.smv[0], c.kmx[:, 4:8]), ((31, 16), c.smk[1], c.smv[1], c.kmx[:, 8:12])]
            else:
                seqs = [(0, T, 'carry')]
                memsets = [((0, T), c.mkT, c.mV, c.kmx[:, 0:4])]
            c.mixer_a(T, seqs, memsets, w_in, pool_w, w_out, xs1[:, cs], [xs2[:, cs]], first_main=(ti_ == 1),
                      poolT_hist=poolT_hist, poolT_out_s=poolT_out_s, poolT_out_p=poolT_out_p, last_main=(ti_ == len(tiles) - 1))
        if "ffn2" in parts:
            c.ffn(T, wg2, wu2, wd2, 4, 5, xs2[:, cs], [xs1[:, cs], xo[:, cs]])
        if "kv" in parts:
            c.kvproj(T, c0, wkv, kT_out, vT_out, lfT_out)
    for q in (c.stq, c.oq, c.sq[0], c.sq[1], c.cq):
        c.wait('sp', (q, q[1]))
    c.st.close()
    return nc


def build_B(nf=NF, tiles=None):
    global R
    R = 4
    c = Ctx(nf)
    nc = c.nc
    TT = TE_B + TP
    dt = lambda name, shape, kind="ExternalInput": nc.dram_tensor(name, shape, F32, kind=kind).ap()
    xT = dt("xT", [D, TT])
    gains = dt("gains", [128, 7 * 32])
    wg1 = dt("wg1", [D, DFF]); wu1 = dt("wu1", [D, DFF]); wd1 = dt("wd1", [DFF, D])
    wg2 = dt("wg2", [D, DFF]); wu2 = dt("wu2", [D, DFF]); wd2 = dt("wd2", [DFF, D])
    w_in = dt("w_in", [D, D]); w_out = dt("w_out", [D, D])
    memT = dt("memT", [D, 256]); wmem1 = dt("wmem1", [D, 2048])
    cmkT = dt("cmkT", [2, 1024, 256]); cmv = dt("cmv", [2, 256, 1024])
    B = {}
    B["kT_all"] = dt("kT_all", [DPOOL, 16384]); B["v_all"] = dt("v_all", [NH, 128, 128, 128])
    lfk = dt("lfk", [128, 3072]); visx = dt("visx", [4, 128, 3072])
    B["kT_own"] = dt("kT_own", [DPOOL, TP]); B["v_own"] = dt("v_own", [NH, 128, 16, 128]); lfo = dt("lfo", [128, 24 * 16])
    B["kTc"] = dt("kTc", [2, DPOOL, PAST]); B["vc"] = dt("vc", [2, NH, 128, 16, 128]); lfc = dt("lfc", [2, 128, 24 * 16])
    B["kTn"] = dt("kTn", [2, DPOOL, 16]); B["vn"] = dt("vn", [2, DPOOL, 16]); lfn = dt("lfn", [2, 16, 24])
    yo = dt("yo", [D, TT], "ExternalOutput")
    dmk = dt("dmk", [1024, 256], "ExternalOutput"); dmv = dt("dmv", [1024, 256], "ExternalOutput")
    xs1 = nc.dram_tensor("xs1", [D, TT], F32).ap()
    xs2 = nc.dram_tensor("xs2", [D, TT], F32).ap()
    B["qs"] = dt("qs", [DPOOL, 512], "ExternalOutput")
    B["biasD"] = [dt(f"biasD{j}", [128, 3072], "ExternalOutput") for j in range(4)]
    B["biasC"] = [dt(f"biasC{j}", [128, 24 * 16], "ExternalOutput") for j in range(2)]
    c.mkT = c.sb("mkT", [128, 8, 256], BF16); c.mV = c.sb("mV", [128, 2, 1024], BF16)
    c.qm = c.sb("qm", [128, 8, 512], BF16)
    c.pT = c.sb("pT", [128, 3, 512], BF16)
    c.mrow = c.sb("mrow", [1, 512], BF16)
    c.rl = c.rstd
    c.kmx = c.sb("kmx", [128, 12], F32)
    c.ident = c.sb("ident", [128, 128], BF16)
    c.triu = c.sb("triu", [128, 128], BF16)
    c.tri = c.triu
    c.qh = c.sb("qh", [128, 2, 512], BF16)
    c.kd = c.sb("kd", [128, 2, 512], BF16)
    c.vd = c.sb("vd", [128, 2, 4, 128], BF16)
    c.bias_sb = c.sb("bias_sb", [128, 2, 128], F32)
    c.kvq2 = [c.sem("kvq2a"), c.sem("kvq2b")]
    c.kvq4 = [c.sem(f"kvq4{i}") for i in range(4)]
    c.small_free = [None, None]
    c.ring_free = [None] * 4
    c.dbias = c.sb("dbias", [128, 4 * 96], F32)
    c.dbias_s = c.sb("dbias_s", [128, 48], F32)
    c.dbias_bf = c.sb("dbias_bf", [128, 4 * 96], BF16)
    c.mrow32 = c.sg[:, 0, :]
    c.m32_free = None
    smb = c.acc[:, 12288:16384].bitcast(BF16)
    c.smk = [smb[:, i * 2048:(i + 1) * 2048].rearrange("p (c s) -> p c s", s=256) for i in range(2)]
    c.smv = [smb[:, 4096 + i * 2048:4096 + (i + 1) * 2048].rearrange("p (k f) -> p k f", f=1024) for i in range(2)]
    c.kvq = c.sem("kvq")
    c.pT_free = [None, None, None]; c.mrow_free = None; c.qm_free = None; c.kv_free = None; c.qh_free = None
    c.consts(gains)
    ti = c.mark('pool', nc.gpsimd.affine_select(out=c.triu[:], in_=c.ones[:], pattern=[[1, 128]], compare_op=ALU.is_ge, fill=0.0, base=0, channel_multiplier=-1))
    ti = c.mark('pool', nc.gpsimd.affine_select(out=c.ident[:], in_=c.ones[:], pattern=[[-1, 128]], compare_op=ALU.is_equal, fill=0.0, base=0, channel_multiplier=1))
    for e in ('pe', 'dve', 'act'):
        c.wait(e, ti)
    c.psb = c.ps[7].bitcast(BF16)
    for j in range(4):
        c.bias_tables(lfk, visx[j], 128, B["biasD"][j])
    for i in range(2):
        c.bias_tables(lfc[i], None, 16, B["biasC"][i])
    c.diag_bias(lfo, 128, 16, 4, c.dbias[:].rearrange("p (h k) -> p h k", k=16))
    for i in range(2):
        c.diag_bias(lfn[i], 16, 1, 1, c.dbias_s[:, i * 24:(i + 1) * 24].rearrange("p (h k) -> p h k", k=1))
    tdb = c.mark('dve', nc.vector.tensor_copy(out=c.dbias_bf[:], in_=c.dbias[:]))
    c.wait('pe', tdb)
    dbg1 = dt("dbg_dbias", [128, 384], "ExternalOutput"); dbg2 = dt("dbg_dbias_s", [128, 48], "ExternalOutput")
    c.dma(c.oq, 'sp', dbg1, c.dbias[:]); c.dma(c.oq, 'sp', dbg2, c.dbias_s[:])
    c.setup_mem(memT, wmem1, 6, dmk, dmv)
    c.kmax(c.mkT, c.kmx[:, 0:4])
    c.wait('pool', c.acc_free)
    c.wait('pool', c.hT_ready)
    for i in range(2):
        c.dma(c.cq, 'pool', c.smk[i], cmkT[i].rearrange("(c p) s -> p c s", p=128))
        t2 = c.dma(c.cq, 'pool', c.smv[i], cmv[i].rearrange("(k p) f -> p k f", p=128))
    for e in ('pe', 'act', 'dve'):
        c.wait(e, t2)
    c.kmax(c.smk[0], c.kmx[:, 4:8])
    c.kmax(c.smk[1], c.kmx[:, 8:12])
    if tiles is None:
        tiles = [(0, TE_B)] + [(TE_B + 512 * i, 512) for i in range(4)]
    for ti_, (c0, T) in enumerate(tiles):
        is_e = (c0 == 0)
        cs = slice(c0, c0 + T)
        c.load_x(T, xT[:, cs])
        c.ffn(T, wg1, wu1, wd1, 0, 1, xT[:, cs], [xs1[:, cs]])
        c.mixer_b(T, is_e, ti_ - 1, w_in, w_out, xs1[:, cs], [xs2[:, cs]], B)
        c.ffn(T, wg2, wu2, wd2, 4, 5, xs2[:, cs], [yo[:, cs]])
    for q in (c.stq, c.oq, c.sq[0], c.sq[1], c.cq, c.kvq):
        c.wait('sp', (q, q[1]))
    c.st.close()
    return nc


def _gl(rows):
    g = np.zeros((7, D), np.float32)
    for k, r in enumerate(rows):
        if r is not None:
            g[k] = r
    return np.ascontiguousarray(g.reshape(7, 32, 128).transpose(2, 0, 1).reshape(128, 224))


def prep_A(inp, c):
    f = lambda a: np.ascontiguousarray(np.asarray(a, dtype=np.float32))
    xp = inp["x_prompt"][0]
    xs = inp["x_sample"]
    halo = np.zeros((15, D), np.float32) if c == 0 else xp[TP * c - 15:TP * c]
    rows = np.concatenate([halo, xs[2 * c], xs[2 * c + 1], xp[TP * c:TP * (c + 1)]], 0)
    ng = inp["norm_g"]
    meta = np.zeros((128, 80), np.float32)
    for gi, w in enumerate((2, 4, 8, 16)):
        for t in range(16):
            pos = TP * c + t
            meta[:, gi * 16 + t] = 1.0 / min(pos + 1, w)
    meta[:, 64] = 0.0 if c == 0 else 1.0
    cmk = inp["cache_mem_k"][0, 2 * c:2 * c + 2].reshape(2, 256, 1024)
    return {
        "xT": f(rows.T),
        "gains": _gl([ng[0, k] for k in range(6)] + [inp["g_kv"]]),
        "gmem": _gl([inp["g_mem"][0], inp["g_mem"][1]]),
        "wg1": f(inp["w_ffn_gate"][0, 0]), "wu1": f(inp["w_ffn_up"][0, 0]), "wd1": f(inp["w_ffn_down"][0, 0]),
        "wg2": f(inp["w_ffn_gate"][0, 1]), "wu2": f(inp["w_ffn_up"][0, 1]), "wd2": f(inp["w_ffn_down"][0, 1]),
        "w_in": f(inp["w_in_a"][0]), "pool_w": f(inp["pool_w"][0]), "w_out": f(inp["w_out_a"][0]),
        "psc": f(inp["pool_scale"][0].reshape(24, 128).T),
        "wkv": f(inp["w_kv"]), "wf": f(inp["w_f"]), "bf": f(inp["b_f"].reshape(NH, 1)),
        "memT": f(inp["mem_prompt"][0].T), "wmem0": f(inp["w_mem_kv"][0]), "wmem1": f(inp["w_mem_kv"][1]),
        "cmkT": f(cmk.transpose(0, 2, 1)), "cmv": f(inp["cache_mem_v"][0, 2 * c:2 * c + 2].reshape(2, 256, 1024)),
        "poolT_hist": f(inp["state_pool"][0, 2 * c:2 * c + 2].transpose(2, 0, 1)),
        "meta": meta,
    }


def _hk(lf, nk):
    return np.ascontiguousarray(lf.reshape(nk, 128, NH).transpose(1, 2, 0).reshape(128, NH * nk).astype(np.float32))


def tiles_of(c):
    return [c, 15 - c, 16 + c, 31 - c]


def _hp(v_tok, nk):
    return np.ascontiguousarray(v_tok.reshape(nk, 128, NH, 128).transpose(2, 1, 0, 3).astype(np.float32))


def prep_B(inp, resA, c, kT_all, v_hp, lfk):
    f = lambda a: np.ascontiguousarray(np.asarray(a, dtype=np.float32))
    ng = inp["norm_g"]
    r = resA[c]
    taus = tiles_of(c)

    def cols(name, tau):
        off = TE_A + (tau % 4) * 512
        return resA[tau // 4][name][:, off:off + 512]
    visx = np.zeros((4, 128, NH, 128), np.float32)
    for j in range(4):
        visx[j, :, :, :4 * taus[j]] = 1.0
    cmk = inp["cache_mem_k"][1, 2 * c:2 * c + 2].reshape(2, 256, 1024)
    vT_own = np.concatenate([cols("vT_out", t) for t in taus], 1)
    m = {
        "xT": f(np.concatenate([r["xo"][:, 15:TE_A]] + [cols("xo", t) for t in taus], 1)),
        "gains": _gl([ng[1, k] for k in range(6)] + [inp["g_mem"][1]]),
        "wg1": f(inp["w_ffn_gate"][1, 0]), "wu1": f(inp["w_ffn_up"][1, 0]), "wd1": f(inp["w_ffn_down"][1, 0]),
        "wg2": f(inp["w_ffn_gate"][1, 1]), "wu2": f(inp["w_ffn_up"][1, 1]), "wd2": f(inp["w_ffn_down"][1, 1]),
        "w_in": f(inp["w_in_b"][0]), "w_out": f(inp["w_out_b"][0]),
        "memT": f(inp["mem_prompt"][0].T), "wmem1": f(inp["w_mem_kv"][1]),
        "cmkT": f(cmk.transpose(0, 2, 1)), "cmv": f(inp["cache_mem_v"][1, 2 * c:2 * c + 2].reshape(2, 256, 1024)),
        "kT_all": kT_all, "v_all": v_hp, "lfk": lfk, "visx": f(visx.reshape(4, 128, 3072)),
        "kT_own": f(np.concatenate([cols("kT_out", t) for t in taus], 1)), "v_own": _hp(f(vT_own.T), 16),
        "lfo": _hk(f(np.concatenate([cols("lfT_out", t) for t in taus], 1).T), 16),
        "kTc": f(inp["cache_fox_k"][2 * c:2 * c + 2].reshape(2, PAST, DPOOL).transpose(0, 2, 1)),
        "vc": np.stack([_hp(f(inp["cache_fox_v"][2 * c + s_].reshape(PAST, DPOOL)), 16) for s_ in range(2)]),
        "lfc": np.stack([_hk(f(inp["cache_fox_logf"][2 * c + s_]), 16) for s_ in range(2)]),
        "kTn": f(np.stack([r["kT_out"][:, 15 + 16 * s_:31 + 16 * s_] for s_ in range(2)])),
        "vn": f(np.stack([r["vT_out"][:, 15 + 16 * s_:31 + 16 * s_] for s_ in range(2)])),
        "lfn": f(np.stack([r["lfT_out"][:, 15 + 16 * s_:31 + 16 * s_].T for s_ in range(2)])),
    }
    return m


def kernel(**inputs):
    inp = {k: np.asarray(v) for k, v in inputs.items()}
    f = lambda a: np.ascontiguousarray(np.asarray(a, dtype=np.float32))
    ncA = build_A()
    resA = run_bass_kernel_spmd(ncA, [prep_A(inp, c) for c in range(NCORES)], core_ids=list(range(NCORES))).results
    kT_all = f(np.concatenate([r["kT_out"][:, TE_A:] for r in resA], 1))
    v_all = _hp(f(np.concatenate([r["vT_out"][:, TE_A:].T for r in resA], 0)), 128)
    lf_all = f(np.concatenate([r["lfT_out"][:, TE_A:].T for r in resA], 0))
    lfk = _hk(lf_all, 128)
    ng = inp["norm_g"]
    mapsB = [prep_B(inp, resA, c, kT_all, v_all, lfk) for c in range(NCORES)]
    ncB = build_B()
    resB = run_bass_kernel_spmd(ncB, mapsB, core_ids=list(range(NCORES))).results
    S = 16384
    y_p = np.zeros((1, S, D), np.float32); y_s = np.zeros((16, 16, D), np.float32)
    k_p = np.zeros((1, S, NH, 128), np.float32); v_p = np.zeros((1, S, NH, 128), np.float32); lf_p = np.zeros((1, S, NH), np.float32)
    k_s = np.zeros((16, 16, NH, 128), np.float32); v_s = np.zeros((16, 16, NH, 128), np.float32); lf_s = np.zeros((16, 16, NH), np.float32)
    mk = np.zeros((2, 1, 256, 4, 256), np.float32); mv = np.zeros((2, 1, 256, 4, 256), np.float32)
    pool_p = np.zeros((1, 1, 15, DPOOL), np.float32); pool_s = np.zeros((1, 16, 15, DPOOL), np.float32)
    for c in range(NCORES):
        ra, rb = resA[c], resB[c]
        sl = slice(TP * c, TP * (c + 1))
        for j, tau in enumerate(tiles_of(c)):
            y_p[0, 512 * tau:512 * (tau + 1)] = rb["yo"][:, TE_B + 512 * j:TE_B + 512 * (j + 1)].T
        k_p[0, sl] = ra["kT_out"][:, TE_A:].T.reshape(TP, NH, 128)
        v_p[0, sl] = ra["vT_out"][:, TE_A:].T.reshape(TP, NH, 128)
        lf_p[0, sl] = ra["lfT_out"][:, TE_A:].T
        for s_ in range(2):
            b = 2 * c + s_
            y_s[b] = rb["yo"][:, 16 * s_:16 * s_ + 16].T
            k_s[b] = ra["kT_out"][:, 15 + 16 * s_:31 + 16 * s_].T.reshape(16, NH, 128)
            v_s[b] = ra["vT_out"][:, 15 + 16 * s_:31 + 16 * s_].T.reshape(16, NH, 128)
            lf_s[b] = ra["lfT_out"][:, 15 + 16 * s_:31 + 16 * s_].T
            pool_s[0, b] = ra["poolT_out_s"][:, s_, :].T
    for l in range(2):
        mk[l, 0] = resA[0]["memk%d" % l].T.reshape(256, 4, 256)
        mv[l, 0] = resA[0]["memv%d" % l].T.reshape(256, 4, 256)
    pool_p[0, 0] = resA[NCORES - 1]["poolT_out_p"].T
    return (y_p, y_s, k_p, v_p, lf_p, mk, mv, pool_p, k_s, v_s, lf_s, pool_s)
```
